# Optimizing a Trainium2 kernel written in Bass

```python
import jax, jax.numpy as jnp
from jax import lax
import numpy as np

D_MODEL = 1024
BATCH = 1
SEQ = 16384
DEPTH = 2

HG_HEADS = 4
HG_DK = 128
HG_DV = 128
HG_WIDTH = HG_HEADS * HG_DV
SA_HEADS = 4
SA_DH = 64
SA_WIDTH = SA_HEADS * SA_DH
MEM_HEADS = 4
MEM_DH = 64
MEM_WIDTH = MEM_HEADS * MEM_DH
MIX_WIDTH = HG_WIDTH + SA_WIDTH + MEM_WIDTH
IDX_HEADS = 8
IDX_DH = 64
TOPK_MAX = 256
N_MEM = 256
D_FF = -(-8 * D_MODEL // (3 * 256)) * 256
CHUNK = 64
Q_BLOCK = 128
EPS = 1e-6
IN_SIZES = (HG_HEADS * HG_DK, HG_HEADS * HG_DK, HG_WIDTH, HG_WIDTH,
            SA_WIDTH, SA_WIDTH, SA_WIDTH,
            IDX_HEADS * IDX_DH, IDX_DH, IDX_HEADS,
            MEM_WIDTH)
IN_WIDTH = sum(IN_SIZES)

kernel_name = "hgrn2_dsa_memory_hybrid_block"

f32 = jnp.float32


def rmsnorm(x, gain):
    x32 = x.astype(f32)
    y = x32 * lax.rsqrt(jnp.mean(x32 * x32, axis=-1, keepdims=True) + EPS)
    return (y * gain.astype(f32)).astype(x.dtype)


def split_in(p):
    offs, acc = [], 0
    for s in IN_SIZES[:-1]:
        acc += s
        offs.append(acc)
    return jnp.split(p, offs, axis=-1)


def to_chunks(a, B, N):
    return a.reshape(B, N, CHUNK, a.shape[2], a.shape[3]).transpose(1, 0, 3, 2, 4)


def hgrn2_mix(q, f_logit, i, g, lb, out_gain):
    B, L = q.shape[:2]
    N = L // CHUNK
    lb = lb.reshape(HG_HEADS, HG_DK)
    z = f_logit.astype(f32)
    log_f = jnp.logaddexp(jnp.log(lb), jnp.log1p(-lb) + jax.nn.log_sigmoid(z))
    k = (1.0 - lb) * jax.nn.sigmoid(-z)
    qf = jax.nn.silu(q.astype(f32))
    xs = (to_chunks(qf, B, N), to_chunks(k, B, N),
          to_chunks(i.astype(f32), B, N), to_chunks(log_f, B, N))
    causal = jnp.tril(jnp.ones((CHUNK, CHUNK), dtype=bool))

    def step(S, inp):
        qc, kc, vc, gc = inp
        b = jnp.cumsum(gc, axis=2)
        o_inter = jnp.einsum('bhck,bhkv->bhcv', qc * jnp.exp(b), S)
        diff = b[:, :, :, None, :] - b[:, :, None, :, :]
        decay = jnp.exp(jnp.where(causal[:, :, None], diff, -jnp.inf))
        A = jnp.einsum('bhtk,bhtsk,bhsk->bhts', qc, decay, kc)
        o = o_inter + jnp.einsum('bhts,bhsv->bhtv', A, vc)
        b_last = b[:, :, -1:, :]
        S = jnp.exp(b_last[:, :, 0, :])[..., None] * S + \
            jnp.einsum('bhck,bhcv->bhkv', kc * jnp.exp(b_last - b), vc)
        return S, o

    S0 = jnp.zeros((B, HG_HEADS, HG_DK, HG_DV), f32)
    _, o = lax.scan(step, S0, xs)
    o = o.transpose(1, 0, 3, 2, 4).reshape(B, L, HG_HEADS, HG_DV)
    o = rmsnorm(o, out_gain) * jax.nn.silu(g.astype(f32))
    return o.reshape(B, L, HG_WIDTH).astype(q.dtype)


def dsa_mix(q, k, v, iq, ik, iw, q_gain, k_gain):
    B, L = q.shape[:2]
    topk = min(TOPK_MAX, L // 4)
    nb = L // Q_BLOCK
    q = rmsnorm(q, q_gain).astype(f32)
    k = rmsnorm(k, k_gain).astype(f32)
    v32 = v.astype(f32)
    ik32 = ik.astype(f32)
    iw32 = iw.astype(f32) * (IDX_HEADS ** -0.5 * IDX_DH ** -0.5)
    key_pos = jnp.arange(L)
    bidx = jnp.arange(B)[:, None, None]

    def blockify(a):
        return a.reshape(B, nb, Q_BLOCK, *a.shape[2:]).swapaxes(0, 1)

    def one_block(args):
        blk, qb, iqb, iwb = args
        qpos = blk * Q_BLOCK + jnp.arange(Q_BLOCK)
        s = jnp.einsum('bqhd,bsd->bqhs', iqb.astype(f32), ik32)
        score = jnp.einsum('bqh,bqhs->bqs', iwb, jax.nn.relu(s))
        visible = key_pos[None, :] <= qpos[:, None]
        score = jnp.where(visible[None], score, -jnp.inf)
        _, idx = lax.top_k(score, topk)
        valid = idx <= qpos[None, :, None]
        kg = k[bidx, idx]
        vg = v32[bidx, idx]
        logits = jnp.einsum('bqhd,bqkhd->bhqk', qb, kg) * (SA_DH ** -0.5)
        logits = jnp.where(valid[:, None], logits, -jnp.inf)
        p = jax.nn.softmax(logits, axis=-1)
        return jnp.einsum('bhqk,bqkhd->bqhd', p, vg)

    out = lax.map(one_block, (jnp.arange(nb), blockify(q), blockify(iq), blockify(iw32)))
    return out.swapaxes(0, 1).reshape(B, L, SA_WIDTH).astype(v.dtype)


def mem_mix(qm, mem_n, w_kv, q_gain, k_gain):
    B, L = qm.shape[:2]
    km, vm = jnp.split(mem_n @ w_kv, 2, axis=-1)
    M = mem_n.shape[1]
    km = km.reshape(B, M, MEM_HEADS, MEM_DH)
    vm = vm.reshape(B, M, MEM_HEADS, MEM_DH)
    qm = rmsnorm(qm, q_gain).astype(f32)
    km = rmsnorm(km, k_gain).astype(f32)
    logits = jnp.einsum('bqhd,bmhd->bhqm', qm, km) * (MEM_DH ** -0.5)
    p = jax.nn.softmax(logits, axis=-1)
    out = jnp.einsum('bhqm,bmhd->bqhd', p, vm.astype(f32))
    return out.reshape(B, L, MEM_WIDTH).astype(mem_n.dtype)


def setup_inputs(seed: int = 0) -> dict:
    key = jax.random.key(seed)
    ks = jax.random.split(key, 16)
    nrm = jax.random.normal

    def gain(k, n):
        return 1.0 + 0.02 * nrm(k, (DEPTH, n), f32)

    return {
        "x": nrm(ks[0], (BATCH, SEQ, D_MODEL), f32),
        "mem": nrm(ks[1], (BATCH, N_MEM, D_MODEL), f32),
        "w_in": nrm(ks[2], (DEPTH, D_MODEL, IN_WIDTH), f32) * D_MODEL ** -0.5,
        "w_out": nrm(ks[3], (DEPTH, MIX_WIDTH, D_MODEL), f32) * MIX_WIDTH ** -0.5,
        "w_mem_kv": nrm(ks[4], (DEPTH, D_MODEL, 2 * MEM_WIDTH), f32) * D_MODEL ** -0.5,
        "lb_logits": nrm(ks[5], (DEPTH, HG_HEADS * HG_DK), f32),
        "norm_mix": gain(ks[6], D_MODEL),
        "norm_mem": gain(ks[7], D_MODEL),
        "norm_ffn": gain(ks[8], D_MODEL),
        "hg_out_gain": gain(ks[9], HG_DV),
        "sa_q_gain": gain(ks[10], SA_DH),
        "sa_k_gain": gain(ks[11], SA_DH),
        "mem_q_gain": gain(ks[12], MEM_DH),
        "mem_k_gain": gain(ks[13], MEM_DH),
        "w_ffn_in": nrm(ks[14], (DEPTH, D_MODEL, 2 * D_FF), f32) * D_MODEL ** -0.5,
        "w_ffn_out": nrm(ks[15], (DEPTH, D_FF, D_MODEL), f32) * D_FF ** -0.5,
    }


def reference(x, mem, w_in, w_out, w_mem_kv, lb_logits, norm_mix, norm_mem, norm_ffn,
              hg_out_gain, sa_q_gain, sa_k_gain, mem_q_gain, mem_k_gain,
              w_ffn_in, w_ffn_out):
    B, L, _ = x.shape
    lb = jnp.cumsum(jax.nn.softmax(lb_logits.astype(f32), axis=0), axis=0)
    lb = lb - lb[0:1]
    for layer in range(DEPTH):
        h = rmsnorm(x, norm_mix[layer])
        hq, hf, hi, hg, sq, sk, sv, iq, ik, iw, mq = split_in(h @ w_in[layer])
        o_hg = hgrn2_mix(hq.reshape(B, L, HG_HEADS, HG_DK), hf.reshape(B, L, HG_HEADS, HG_DK),
                         hi.reshape(B, L, HG_HEADS, HG_DV), hg.reshape(B, L, HG_HEADS, HG_DV),
                         lb[layer], hg_out_gain[layer])
        o_sa = dsa_mix(sq.reshape(B, L, SA_HEADS, SA_DH), sk.reshape(B, L, SA_HEADS, SA_DH),
                       sv.reshape(B, L, SA_HEADS, SA_DH), iq.reshape(B, L, IDX_HEADS, IDX_DH),
                       ik, iw, sa_q_gain[layer], sa_k_gain[layer])
        mem_n = rmsnorm(mem, norm_mem[layer])
        o_mem = mem_mix(mq.reshape(B, L, MEM_HEADS, MEM_DH), mem_n, w_mem_kv[layer],
                        mem_q_gain[layer], mem_k_gain[layer])
        mixed = jnp.concatenate([o_hg, o_sa, o_mem], axis=-1)
        x = x + (mixed @ w_out[layer]).astype(x.dtype)
        h = rmsnorm(x, norm_ffn[layer])
        a, b = jnp.split(h @ w_ffn_in[layer], 2, axis=-1)
        x = x + ((jax.nn.silu(a) * b) @ w_ffn_out[layer]).astype(x.dtype)
    return x
```

```python
import contextlib
import numpy as np
import concourse.bass as bass
import concourse.mybir as mybir
from concourse.bass_utils import run_bass_kernel_spmd

F32 = mybir.dt.float32
BF16 = mybir.dt.bfloat16
AF = mybir.ActivationFunctionType
ALU = mybir.AluOpType
AX = mybir.AxisListType

ENGS = ("pe", "act", "dve", "pool", "sp")


class Prog:
    def __init__(self, nc):
        self.nc = nc
        self.ops = []
        self.same_engine_sync = True
        self.max_outstanding = 3

    def op(self, eng, fn, reads=(), writes=(), dma=False):
        self.ops.append((eng, fn, tuple(reads), tuple(writes), dma))

    def pe(self, fn, r=(), w=()): self.op("pe", fn, r, w)
    def act(self, fn, r=(), w=()): self.op("act", fn, r, w)
    def dve(self, fn, r=(), w=()): self.op("dve", fn, r, w)
    def pool(self, fn, r=(), w=()): self.op("pool", fn, r, w)
    def dma(self, fn, r=(), w=(), q="sp"): self.op(q, fn, r, w, True)

    def emit(self):
        nc = self.nc
        ops = self.ops
        n = len(ops)
        R = self.max_outstanding
        stream = []
        prev_same = [None] * n if False else None
        qcount = {}
        for (e, _, _, _, d) in ops:
            if d:
                k = qcount.get(e, 0); qcount[e] = k + 1
                stream.append("%s_dma%d" % (e, k % R))
            else:
                stream.append(e)
        last_w = {}
        readers = {}
        deps = [set() for _ in range(n)]
        for i, (e, fn, rs, ws, d) in enumerate(ops):
            for k in rs:
                if k in last_w:
                    deps[i].add(last_w[k])
            for k in ws:
                if k in last_w:
                    deps[i].add(last_w[k])
                for j in readers.get(k, ()):
                    if j != i:
                        deps[i].add(j)
            for k in rs:
                readers.setdefault(k, []).append(i)
            for k in ws:
                last_w[k] = i
                readers[k] = []
        needed = [False] * n
        for i in range(n):
            keep = set()
            for j in deps[i]:
                if stream[j] == stream[i] == "pe":
                    continue
                if (not self.same_engine_sync) and stream[j] == stream[i] and "_dma" not in stream[i]:
                    continue
                keep.add(j)
            deps[i] = keep
        for i in range(n):
            for j in deps[i]:
                needed[j] = True
        for i in range(n):
            if ops[i][4]:
                needed[i] = True
        cnt = {}
        val = [0] * n
        for i in range(n):
            s = stream[i]
            if needed[i]:
                cnt[s] = cnt.get(s, 0) + (16 if "_dma" in s else 1)
            val[i] = cnt.get(s, 0)
        streams = sorted(set(stream))
        self.final_counts = dict(cnt)
        import contextlib
        with contextlib.ExitStack() as st:
            sems = {s: st.enter_context(nc.semaphore("sem_" + s)) for s in streams}
            block = st.enter_context(nc.Block())
            per_eng = {e: [i for i in range(n) if ops[i][0] == e] for e in ENGS}

            def run(engname, eng):
                waited = {}
                ndma = 0
                for i in per_eng[engname]:
                    if ops[i][4]:
                        s_ = stream[i]
                        lim = val[i] - 16
                        if lim > 0 and waited.get(s_, 0) < lim:
                            eng.wait_ge(sems[s_], lim)
                            waited[s_] = lim
                        ndma += 1
                    need = {}
                    for j in deps[i]:
                        s = stream[j]
                        need[s] = max(need.get(s, 0), val[j])
                    for s, v in need.items():
                        if waited.get(s, 0) < v:
                            eng.wait_ge(sems[s], v)
                            waited[s] = v
                    ins = ops[i][1](eng)
                    if needed[i]:
                        ins.then_inc(sems[stream[i]], 16 if ops[i][4] else 1)
                if engname == "sp":
                    for s in streams:
                        if "_dma" in s and cnt.get(s, 0) > 0:
                            eng.wait_ge(sems[s], cnt[s])

            @block.sync
            def _(e): run("sp", e)

            @block.gpsimd
            def _(e): run("pool", e)

            @block.scalar
            def _(e): run("act", e)

            @block.vector
            def _(e): run("dve", e)

            @block.tensor
            def _(e): run("pe", e)

EPS = 1e-6
NT = 2048
NG = 4
IN_W = 3656
C_HQ, C_HF, C_HI, C_HG, C_SQ, C_SK, C_SV, C_IQ, C_IK, C_IW, C_MQ = 0, 512, 1024, 1536, 2048, 2304, 2560, 2816, 3328, 3392, 3400


class _Stop(Exception): pass

def build_A(STOP=99.0):
    nc = bass.Bass("TRN2", target_bir_lowering=False)
    def din(name, shape, dt=F32): return nc.dram_tensor(name, shape, dt, kind="ExternalInput").ap()
    def dout(name, shape, dt=F32): return nc.dram_tensor(name, shape, dt, kind="ExternalOutput").ap()
    x = din("x", [NT, 1024]); w_in = din("w_in", [1024, IN_W]); gmix = din("gmix", [128, 1024])
    mem = din("mem", [256, 1024]); gmem = din("gmem", [128, 1024]); w_kv = din("w_kv", [1024, 512])
    lbl = din("lbl", [128, 8]); lsel = din("lsel", [128, 1]); gcols = din("gcols", [128, 4])
    ident = din("ident", [128, 128]); bdin = din("bd", [128, 128]); cmaskin = din("cmask", [64, 512]); rmaskin = din("rmask", [128, 512])
    o_intra = dout("o_intra", [NG, 4, 64, 1024]); qhat = dout("qhat", [4, 128, NT], BF16); g_out = dout("g_out", [NT, 512])
    kv_out = dout("kv_out", [NG, 4, 128, 1024]); a_out = dout("a_out", [128, NG * 4 * 8]); omem = dout("omem", [NT, 256])
    QT = dout("QT", [2, 128, NT], BF16); KT = dout("KT", [2, 128, NT], BF16); V = dout("V", [NT, 320], BF16)
    ikT = dout("ikT", [64, NT], BF16); iqT = dout("iqT", [64, 8, NT], BF16); iw_out = dout("iw_out", [NT, 128])
    P = Prog(nc)
    def chk(k):
        if k > STOP: raise _Stop()
    with contextlib.ExitStack() as st:
      try:
          def sb(name, shape, dt=F32): return st.enter_context(nc.sbuf_tensor(name, shape, dt))
          def ps(name, shape, dt=F32): return st.enter_context(nc.psum_tensor(name, shape, dt))
          wbf = sb("wbf", [128, 8, IN_W], BF16)
          gainb = sb("gainb", [128, 1024]); gmemb = sb("gmemb", [128, 1024])
          idb = sb("idb", [128, 128], BF16); bd = sb("bdb", [128, 128], BF16); cmask = sb("cmask_s", [64, 512], BF16); rmask = sb("rmask_s", [128, 512])
          lbt = sb("lbt", [128, 8]); lselt = sb("lselt", [128, 1]); lb = sb("lb", [128, 4]); oml = sb("oml", [128, 4]); gc = sb("gc", [128, 4])
          xin = [sb(f"xin{i}", [128, 1024]) for i in range(2)]
          hb = [sb(f"hb{i}", [128, 1024], BF16) for i in range(2)]
          junk = sb("junk", [128, 1024], BF16); epsb = sb("epsb", [128, 1]); P.dve(lambda e: e.memset(epsb[:], EPS), w=["epsb"])
          ss = sb("ss", [128, 4])
          hT = sb("hT", [128, 8, 512], BF16)
          ptr = ps("ptr", [128, 1024], BF16)
          pf = ps("pf", [128, 512]); pt = ps("pt", [128, 512]); pw = ps("pw", [128, 512])
          pbA = ps("pbA", [128, 1024]); pbB = ps("pbB", [128, 1024])
          for kc in range(8):
              P.dma(lambda e, kc=kc: e.dma_start(out=wbf[:, kc, :], in_=w_in[kc * 128:(kc + 1) * 128, :]), w=["wbf"], q="pool")
          P.dma(lambda e: e.dma_start(out=gainb[:], in_=gmix[:, :]), w=["gainb"])
          P.dma(lambda e: e.dma_start(out=gmemb[:], in_=gmem[:, :]), w=["gmemb"])
          P.dma(lambda e: e.dma_start(out=idb[:], in_=ident[:, :]), w=["idb"], q="pool")
          P.dma(lambda e: e.dma_start(out=bd[:], in_=bdin[:, :]), w=["bd"], q="pool")
          P.dma(lambda e: e.dma_start(out=cmask[:], in_=cmaskin[:, :]), w=["cmask"], q="pool")
          P.dma(lambda e: e.dma_start(out=rmask[:], in_=rmaskin[:, :]), w=["rmask"])
          P.dma(lambda e: e.dma_start(out=lbt[:], in_=lbl[:, :]), w=["lbt"])
          P.dma(lambda e: e.dma_start(out=lselt[:], in_=lsel[:, :]), w=["lselt"])
          P.dma(lambda e: e.dma_start(out=gc[:], in_=gcols[:, :]), w=["gc"])
          P.dve(lambda e: e.tensor_sub(out=lb[:], in0=lbt[:, 4:8], in1=lbt[:, 0:4]), r=["lbt"], w=["lb"])
          P.act(lambda e: e.activation(out=lb[:], in_=lb[:], func=AF.Sigmoid), r=["lb"], w=["lb"])
          P.dve(lambda e: e.tensor_scalar(out=lb[:], in0=lb[:], scalar1=lselt[:, 0:1], scalar2=None, op0=ALU.mult), r=["lb", "lselt"], w=["lb"])
          P.dve(lambda e: e.tensor_scalar(out=oml[:], in0=lb[:], scalar1=-1.0, scalar2=1.0, op0=ALU.mult, op1=ALU.add), r=["lb"], w=["oml"])
          P.dve(lambda e: e.tensor_scalar(out=gc[:, 0:1], in0=gc[:, 0:1], scalar1=0.125, scalar2=None, op0=ALU.mult), r=["gc"], w=["gc"])
          P.dve(lambda e: e.tensor_scalar(out=gc[:, 2:3], in0=gc[:, 2:3], scalar1=0.125, scalar2=None, op0=ALU.mult), r=["gc"], w=["gc"])

          def rms_to_hb(src, srck, gtile, gk, dst, dstk):
              P.act(lambda e: e.activation(out=junk[:], in_=src, func=AF.Square, accum_out=ss[:, 0:1]), r=[srck], w=["junk", "ss"])
              P.act(lambda e: e.activation(out=ss[:, 1:2], in_=ss[:, 0:1], func=AF.Sqrt, bias=epsb[:, 0:1], scale=1.0 / 1024), r=["ss", "epsb"], w=["ss"])
              P.dve(lambda e: e.reciprocal(out=ss[:, 1:2], in_=ss[:, 1:2]), r=["ss"], w=["ss"])
              P.dve(lambda e: e.scalar_tensor_tensor(out=dst, in0=src, scalar=ss[:, 1:2], in1=gtile, op0=ALU.mult, op1=ALU.mult), r=[srck, "ss", gk], w=[dstk])

          def transpose8(srcb, srck, dst3, dstk):
              for kc in range(8):
                  P.pe(lambda e, kc=kc: e.transpose(ptr[:, kc * 128:(kc + 1) * 128], srcb[:, kc * 128:(kc + 1) * 128], idb[:]), r=[srck, "idb"], w=["ptr"])
              P.act(lambda e: e.copy(out=dst3, in_=ptr[:].rearrange("p (k t) -> p k t", k=8)), r=["ptr"], w=[dstk])

          def head_norm(psrc, psk, gcol, dst, dstk, tmpa, tmpb, n=512):
              P.act(lambda e: e.activation(out=tmpa, in_=psrc, func=AF.Square), r=[psk], w=["hn_a"])
              P.pe(lambda e: e.matmul(pw[:, 0:n], bd[:], tmpa, start=True, stop=True), r=["hn_a", "bd"], w=["pw"])
              P.act(lambda e: e.activation(out=tmpb, in_=pw[:, 0:n], func=AF.Sqrt, bias=epsb[:, 0:1], scale=1.0), r=["pw", "epsb"], w=["hn_b"])
              P.dve(lambda e: e.reciprocal(out=tmpb, in_=tmpb), r=["hn_b"], w=["hn_b"])
              P.dve(lambda e: e.scalar_tensor_tensor(out=dst, in0=psrc, scalar=gc[:, gcol:gcol + 1], in1=tmpb, op0=ALU.mult, op1=ALU.mult), r=[psk, "hn_b", "gc"], w=[dstk])

          hn_a = sb("hn_a", [128, 512], BF16); hn_b = sb("hn_b", [128, 512])
          chk(1)
          wkv = sb("wkv", [128, 8, 512], BF16)
          memT = sb("memT", [128, 8, 256], BF16)
          kmT = sb("kmT", [128, 2, 256], BF16)
          vm1 = sb("vm1", [128, 2, 4, 80], BF16)
          for kc in range(8):
              P.dma(lambda e, kc=kc: e.dma_start(out=wkv[:, kc, :], in_=w_kv[kc * 128:(kc + 1) * 128, :]), w=["wkv"], q="pool")
          for mt in range(2):
              P.dma(lambda e, mt=mt: e.dma_start(out=xin[mt][:], in_=mem[mt * 128:(mt + 1) * 128, :]), w=[f"xin{mt}"])
              rms_to_hb(xin[mt][:], f"xin{mt}", gmemb[:], "gmemb", hb[mt][:], f"hb{mt}")
              transpose8(hb[mt], f"hb{mt}", memT[:, :, mt * 128:(mt + 1) * 128], "memT")
          for hp in range(2):
              for kc in range(8):
                  P.pe(lambda e, kc=kc, hp=hp: e.matmul(pf[:, 0:256], wkv[:, kc, hp * 128:(hp + 1) * 128], memT[:, kc, :], start=(kc == 0), stop=(kc == 7)), r=["wkv", "memT"], w=["pf"])
              head_norm(pf[:, 0:256], "pf", 3, kmT[:, hp, :], "kmT", hn_a[:, 0:256], hn_b[:, 0:256], n=256)
          P.dve(lambda e: e.memset(vm1[:], 1.0), w=["vm1"])
          for mt in range(2):
              for kc in range(8):
                  P.pe(lambda e, kc=kc, mt=mt: e.matmul(pt[:, 0:256], memT[:, kc, mt * 128:(mt + 1) * 128], wkv[:, kc, 256:512], start=(kc == 0), stop=(kc == 7)), r=["wkv", "memT"], w=["pt"])
              P.act(lambda e, mt=mt: e.copy(out=vm1[:, mt, :, 0:64], in_=pt[:, 0:256].rearrange("p (h d) -> p h d", h=4)), r=["pt"], w=["vm1"])

          chk(2)
          def w32(name): return sb(name, [128, 512])
          def w16(name): return sb(name, [128, 512], BF16)
          sig = w32("sig"); fg = w32("fg"); lf = w32("lf"); bcum = w32("bcum"); qs = w32("qs"); kk = w32("kk")
          e1 = w32("e1"); e2 = w32("e2"); eb = w32("eb")
          qt = w16("qt"); kt = w16("kt"); qh = w16("qh")
          aall = sb("aall", [128, NG * 4 * 8])
          sm = sb("sm", [128, 5, 8])
          AT = sb("AT", [64, 512], BF16); ktok = sb("ktok", [64, 8, 128], BF16)
          vtok = sb("vtok", [64, 8, 512], BF16)
          oist = sb("oist", [64, 8, 128]); kvst = sb("kvst", [128, 8, 128])
          gst = sb("gst", [128, 512]); vst = sb("vst", [128, 4, 80], BF16); iwst = sb("iwst", [128, 128])
          qn = w16("qn"); mqT = sb("mqT", [128, 2, 512], BF16)
          iqst = sb("iqst", [64, 8, 512], BF16); ikst = sb("ikst", [64, 512], BF16)
          PT = sb("PT", [128, 2, 4, 512], BF16)
          rc = sb("rc", [128, 4]); omst = sb("omst", [128, 4, 64])
          P.dve(lambda e: e.memset(vst[:], 1.0), w=["vst"])
          P.dve(lambda e: e.memset(iwst[:], 0.0), w=["iwst"])

          def proj_f(col0, ncols, bank=pf, bk="pf"):
              for kc in range(8):
                  P.pe(lambda e, kc=kc: e.matmul(bank[0:ncols, :], wbf[:, kc, col0:col0 + ncols], hT[:, kc, :], start=(kc == 0), stop=(kc == 7)), r=["wbf", "hT"], w=[bk])

          def proj_t(t0, m, col0, ncols, out_ap, bk):
              for kc in range(8):
                  P.pe(lambda e, kc=kc: e.matmul(out_ap, hT[:, kc, t0:t0 + m], wbf[:, kc, col0:col0 + ncols], start=(kc == 0), stop=(kc == 7)), r=["wbf", "hT"], w=[bk])

          for g in range(NG):
              T0 = g * 512
              for blk in range(4):
                  xi = blk % 2
                  P.dma(lambda e, xi=xi, blk=blk, T0=T0: e.dma_start(out=xin[xi][:], in_=x[T0 + blk * 128:T0 + (blk + 1) * 128, :]), w=[f"xin{xi}"])
                  rms_to_hb(xin[xi][:], f"xin{xi}", gainb[:], "gainb", hb[xi][:], f"hb{xi}")
                  transpose8(hb[xi], f"hb{xi}", hT[:, :, blk * 128:(blk + 1) * 128], "hT")
              chk(3)
              for c in range(8):
                  proj_t(c * 64, 64, C_HI, 512, pt[0:64, :], "pt")
                  P.act(lambda e, c=c: e.copy(out=vtok[:, c, :], in_=pt[0:64, :]), r=["pt"], w=["vtok"])
              chk(4)
              import os
              for h in range(0 if os.environ.get('SKIP4') else 4):
                  proj_f(C_HF + h * 128, 128)
                  P.act(lambda e: e.activation(out=sig[:], in_=pf[:], func=AF.Sigmoid), r=["pf"], w=["sig"])
                  P.dve(lambda e, h=h: e.tensor_scalar(out=fg[:], in0=sig[:], scalar1=oml[:, h:h + 1], scalar2=lb[:, h:h + 1], op0=ALU.mult, op1=ALU.add), r=["sig", "oml", "lb"], w=["fg"])
                  P.act(lambda e: e.activation(out=lf[:], in_=fg[:], func=AF.Ln), r=["fg"], w=["lf"])
                  P.dve(lambda e: e.tensor_scalar(out=kk[:], in0=fg[:], scalar1=-1.0, scalar2=1.0, op0=ALU.mult, op1=ALU.add), r=["fg"], w=["kk"])
                  proj_f(C_HQ + h * 128, 128)
                  P.act(lambda e: e.activation(out=qs[:], in_=pf[:], func=AF.Silu), r=["pf"], w=["qs"])
                  P.dve(lambda e: e.tensor_tensor_scan(out=bcum[:], data0=rmask[:], data1=lf[:], initial=0.0, op0=ALU.mult, op1=ALU.add), r=["rmask", "lf"], w=["bcum"])
                  b3 = bcum[:].rearrange("p (c t) -> p c t", t=64)
                  P.dve(lambda e: e.tensor_scalar(out=sm[:, 0, :], in0=b3[:, :, 31], scalar1=-1.0, scalar2=None, op0=ALU.mult), r=["bcum"], w=["sm0"])
                  P.dve(lambda e: e.tensor_copy(out=sm[:, 1, :], in_=b3[:, :, 31]), r=["bcum"], w=["sm1"])
                  P.dve(lambda e: e.tensor_copy(out=sm[:, 2, :], in_=b3[:, :, 63]), r=["bcum"], w=["sm2"])
                  P.dve(lambda e: e.tensor_sub(out=sm[:, 3, :], in0=sm[:, 2, :], in1=sm[:, 1, :]), r=["sm1", "sm2"], w=["sm3"])
                  for c in range(8):
                      P.act(lambda e, c=c: e.activation(out=e1[:, c * 64:(c + 1) * 64], in_=bcum[:, c * 64:(c + 1) * 64], func=AF.Exp, bias=sm[:, 0, c:c + 1], scale=1.0), r=["bcum", "sm0"], w=["e1"])
                      P.act(lambda e, c=c: e.activation(out=e2[:, c * 64:(c + 1) * 64], in_=bcum[:, c * 64:(c + 1) * 64], func=AF.Exp, bias=sm[:, 1, c:c + 1], scale=-1.0), r=["bcum", "sm1"], w=["e2"])
                  P.act(lambda e: e.activation(out=eb[:], in_=bcum[:], func=AF.Exp), r=["bcum"], w=["eb"])
                  P.act(lambda e: e.activation(out=sm[:, 3, :], in_=sm[:, 3, :], func=AF.Exp), r=["sm3"], w=["sm3"])
                  P.act(lambda e: e.activation(out=sm[:, 4, :], in_=sm[:, 2, :], func=AF.Exp), r=["sm2"], w=["sm4"])
                  P.dve(lambda e: e.tensor_mul(out=qt[:], in0=qs[:], in1=e1[:]), r=["qs", "e1"], w=["qt"])
                  P.dve(lambda e: e.tensor_mul(out=kt[:], in0=kk[:], in1=e2[:]), r=["kk", "e2"], w=["kt"])
                  P.dve(lambda e: e.tensor_mul(out=qh[:], in0=qs[:], in1=eb[:]), r=["qs", "eb"], w=["qh"])
                  P.dma(lambda e, h=h, T0=T0: e.dma_start(out=qhat[h, :, T0:T0 + 512], in_=qh[:]), r=["qh"])
                  P.dve(lambda e, h=h, g=g: e.tensor_copy(out=aall[:, (g * 4 + h) * 8:(g * 4 + h + 1) * 8], in_=sm[:, 4, :]), r=["sm4"], w=["aall"])
                  for c in range(8):
                      P.pe(lambda e, c=c: e.matmul(pw[0:64, c * 64:(c + 1) * 64], kt[:, c * 64:(c + 1) * 64], qt[:, c * 64:(c + 1) * 64], start=True, stop=True), r=["kt", "qt"], w=["pw"])
                  P.dve(lambda e: e.tensor_tensor(out=AT[:], in0=pw[0:64, :], in1=cmask[:], op=ALU.mult), r=["pw", "cmask"], w=["AT"])
                  for c in range(8):
                      P.pe(lambda e, c=c: e.transpose(ptr[0:64, c * 128:(c + 1) * 128], kt[:, c * 64:(c + 1) * 64], idb[:]), r=["kt", "idb"], w=["ptr"])
                  P.act(lambda e: e.copy(out=ktok[:], in_=ptr[0:64, :].rearrange("p (c k) -> p c k", c=8)), r=["ptr"], w=["ktok"])
                  for c in range(8):
                      P.pe(lambda e, c=c, h=h: e.matmul(pbA[0:64, c * 128:(c + 1) * 128], AT[:, c * 64:(c + 1) * 64], vtok[:, c, h * 128:(h + 1) * 128], start=True, stop=True), r=["AT", "vtok"], w=["pbA"])
                  P.act(lambda e: e.copy(out=oist[:], in_=pbA[0:64, :].rearrange("p (c v) -> p c v", c=8)), r=["pbA"], w=["oist"])
                  P.dma(lambda e, h=h, g=g: e.dma_start(out=o_intra[g, h, :, :], in_=oist[:].rearrange("p c v -> p (c v)")), r=["oist"])
                  for c in range(8):
                      P.pe(lambda e, c=c, h=h: e.matmul(pbB[:, c * 128:(c + 1) * 128], ktok[:, c, :], vtok[:, c, h * 128:(h + 1) * 128], start=True, stop=True), r=["ktok", "vtok"], w=["pbB"])
                  P.dve(lambda e: e.tensor_tensor(out=kvst[:], in0=pbB[:].rearrange("p (c v) -> p c v", c=8), in1=sm[:, 3, :].unsqueeze(2).to_broadcast([128, 8, 128]), op=ALU.mult), r=["pbB", "sm3"], w=["kvst"])
                  P.dma(lambda e, h=h, g=g: e.dma_start(out=kv_out[g, h, :, :], in_=kvst[:].rearrange("p c v -> p (c v)")), r=["kvst"], q="pool")
              chk(5)
              for blk in range(4):
                  t0 = blk * 128
                  proj_t(t0, 128, C_HG, 512, pt[:, :], "pt")
                  P.act(lambda e: e.copy(out=gst[:], in_=pt[:]), r=["pt"], w=["gst"])
                  P.dma(lambda e, t0=t0, T0=T0: e.dma_start(out=g_out[T0 + t0:T0 + t0 + 128, :], in_=gst[:]), r=["gst"])
                  chk(5.1)
                  proj_t(t0, 128, C_SV, 256, pt[:, 0:256], "pt")
                  proj_t(t0, 128, C_IK, 72, pt[:, 256:328], "pt")
                  chk(5.2)
                  P.act(lambda e: e.copy(out=vst[:, :, 0:64], in_=pt[:, 0:256].rearrange("p (h d) -> p h d", h=4)), r=["pt"], w=["vst"])
                  P.act(lambda e: e.copy(out=iwst[:, 0:8], in_=pt[:, 320:328]), r=["pt"], w=["iwst"])
                  chk(5.3)
                  P.dma(lambda e, t0=t0, T0=T0: e.dma_start(out=V[T0 + t0:T0 + t0 + 128, :], in_=vst[:].rearrange("p h e -> p (h e)")), r=["vst"])
                  P.dma(lambda e, t0=t0, T0=T0: e.dma_start(out=iw_out[T0 + t0:T0 + t0 + 128, :], in_=iwst[:]), r=["iwst"])
              chk(6)
              for pair in range(2):
                  proj_f(C_SQ + pair * 128, 128)
                  head_norm(pf[:], "pf", 0, qn[:], "qn", hn_a[:], hn_b[:])
                  P.dma(lambda e, pair=pair, T0=T0: e.dma_start(out=QT[pair, :, T0:T0 + 512], in_=qn[:]), r=["qn"])
                  proj_f(C_SK + pair * 128, 128)
                  head_norm(pf[:], "pf", 1, qn[:], "qn", hn_a[:], hn_b[:])
                  P.dma(lambda e, pair=pair, T0=T0: e.dma_start(out=KT[pair, :, T0:T0 + 512], in_=qn[:]), r=["qn"])
                  proj_f(C_MQ + pair * 128, 128)
                  head_norm(pf[:], "pf", 2, mqT[:, pair, :], "mqT", hn_a[:], hn_b[:])
              for ih in range(8):
                  proj_f(C_IQ + ih * 64, 64)
                  P.act(lambda e, ih=ih: e.copy(out=iqst[:, ih, :], in_=pf[0:64, :]), r=["pf"], w=["iqst"])
              P.dma(lambda e, T0=T0: e.dma_start(out=iqT[:, :, T0:T0 + 512], in_=iqst[:]), r=["iqst"])
              proj_f(C_IK, 64)
              P.act(lambda e: e.copy(out=ikst[:], in_=pf[0:64, :]), r=["pf"], w=["ikst"])
              P.dma(lambda e, T0=T0: e.dma_start(out=ikT[:, T0:T0 + 512], in_=ikst[:]), r=["ikst"])
              chk(7)
              for mt in range(2):
                  for h in range(4):
                      po = (h % 2) * 64
                      P.pe(lambda e, mt=mt, h=h, po=po: e.matmul(pw[:], kmT[po:po + 64, h // 2, mt * 128:(mt + 1) * 128], mqT[po:po + 64, h // 2, :], start=True, stop=True), r=["kmT", "mqT"], w=["pw"])
                      P.act(lambda e, mt=mt, h=h: e.activation(out=PT[:, mt, h, :], in_=pw[:], func=AF.Exp), r=["pw"], w=["PT"])
              for blk in range(4):
                  t0 = blk * 128
                  for h in range(4):
                      for mt in range(2):
                          P.pe(lambda e, mt=mt, h=h, t0=t0: e.matmul(pt[:, h * 128:h * 128 + 65], PT[:, mt, h, t0:t0 + 128], vm1[:, mt, h, 0:65], start=(mt == 0), stop=(mt == 1)), r=["PT", "vm1"], w=["pt"])
                  pv = pt[:].rearrange("p (h e) -> p h e", e=128)
                  P.dve(lambda e, pv=pv: e.reciprocal(out=rc[:], in_=pv[:, :, 64]), r=["pt"], w=["rc"])
                  P.dve(lambda e, pv=pv: e.tensor_tensor(out=omst[:], in0=pv[:, :, 0:64], in1=rc[:].unsqueeze(2).to_broadcast([128, 4, 64]), op=ALU.mult), r=["pt", "rc"], w=["omst"])
                  P.dma(lambda e, t0=t0, T0=T0: e.dma_start(out=omem[T0 + t0:T0 + t0 + 128, :], in_=omst[:].rearrange("p h d -> p (h d)")), r=["omst"])
      except _Stop:
        pass
      P.dma(lambda e: e.dma_start(out=a_out[:, :], in_=aall[:]), r=["aall"])
      P.emit()
    return nc


def consts_A():
    ident = np.eye(128, dtype=np.float32)
    bd = np.zeros((128, 128), np.float32); bd[:64, :64] = 1.0 / 64; bd[64:, 64:] = 1.0 / 64
    cm = np.triu(np.ones((64, 64), np.float32))
    cmask = np.tile(cm, (1, 8))
    rmask = np.ones((128, 512), np.float32); rmask[:, ::64] = 0.0
    return dict(ident=ident, bd=bd, cmask=cmask, rmask=rmask)

NSLOT = 16
NITER = 24
RANGE = 512.0


def build_B(NS=NSLOT):
    nc = bass.Bass("TRN2", target_bir_lowering=False)
    def din(name, shape, dt=F32): return nc.dram_tensor(name, shape, dt, kind="ExternalInput").ap()
    def dout(name, shape, dt=F32): return nc.dram_tensor(name, shape, dt, kind="ExternalOutput").ap()
    QT = din("QT", [128, 2, 2048], BF16)
    iqs = din("iqs", [16, 64, 1024], BF16)
    iwp = din("iwp", [128, 128])
    KTg = din("KTg", [128, 2, 16384], BF16)
    Vg = din("Vg", [128, 128, 320], BF16)
    ikT = din("ikT", [64, 16384], BF16)
    CB = din("CB", [128, 1024])
    ident = din("ident", [128, 128])
    a_s = din("a_s", [128, 256]); kv_s = din("kv_s", [128, 64, 256])
    S_out = dout("S_out", [128, 64, 256])
    o_sa = dout("o_sa", [2048, 256])
    P = Prog(nc)
    with contextlib.ExitStack() as st:
        def sb(name, shape, dt=F32): return st.enter_context(nc.sbuf_tensor(name, shape, dt))
        def ps(name, shape, dt=F32): return st.enter_context(nc.psum_tensor(name, shape, dt))
        score = sb("score", [128, 16384])
        junk = sb("junk", [128, 4096], BF16)
        ikt = sb("ikt", [64, 16384], BF16)
        qt = sb("qt", [128, 2, 2048], BF16)
        iq = sb("iq", [64, 1024], BF16)
        iwt = sb("iwt", [128, 128]); sgn = sb("sgn", [128, 128]); absw = sb("absw", [128, 128])
        cb = sb("cb", [128, 1024]); idb = sb("idb", [128, 128], BF16)
        D = sb("D", [128, 8, 128], BF16)
        T = [sb(f"T{h}", [128, 512], BF16) for h in range(8)]
        ktc = sb("ktc", [128, 2, 2048], BF16); vc = sb("vc", [128, 16, 320], BF16)
        mbc = sb("mbc", [128, 512], BF16); pT = sb("pT", [128, 512], BF16)
        sm = sb("sm", [128, 16]); cnt4 = sb("cnt4", [128, 4])
        ost = sb("ost", [128, 4, 64]); rc = sb("rc", [128, 4])
        asb = sb("asb", [128, 256])
        zl = sb("zl", [128, 128], BF16); zr = sb("zr", [128, 512], BF16)
        P.dve(lambda e: e.memset(zl[:], 0.0), w=["zl"])
        P.dve(lambda e: e.memset(zr[:], 0.0), w=["zr"])
        pS = [ps(f"pS{i}", [128, 512]) for i in range(3)]
        pSc = ps("pSc", [128, 512])
        pL = [ps(f"pL{i}", [128, 512]) for i in range(2)]
        pO = ps("pO", [128, 512])
        P.dma(lambda e: e.dma_start(out=ikt[:], in_=ikT[:, :]), w=["ikt"])
        P.dma(lambda e: e.dma_start(out=qt[:], in_=QT[:, :, :]), w=["qt"])
        P.dma(lambda e: e.dma_start(out=iwt[:], in_=iwp[:, :]), w=["iwt"])
        P.dma(lambda e: e.dma_start(out=cb[:], in_=CB[:, :]), w=["cb"])
        P.dma(lambda e: e.dma_start(out=idb[:], in_=ident[:, :]), w=["idb"], q="pool")
        P.dma(lambda e: e.dma_start(out=asb[:], in_=a_s[:, :]), w=["asb"])
        for half in range(2):
            kin = score[:, 0:8192].rearrange("p (v c) -> p v c", c=256)
            kout = score[:, 8192:16384].rearrange("p (v c) -> p v c", c=256)
            P.dma(lambda e, half=half, kin=kin: e.dma_start(out=kin, in_=kv_s[:, half * 32:(half + 1) * 32, :]), w=["score"])
            for v in range(32):
                P.dve(lambda e, v=v: e.tensor_tensor_scan(out=score[:, 8192 + v * 256:8192 + (v + 1) * 256], data0=asb[:], data1=score[:, v * 256:(v + 1) * 256], initial=0.0, op0=ALU.mult, op1=ALU.add), r=["asb", "score"], w=["score"])
            P.dma(lambda e, half=half, kout=kout: e.dma_start(out=S_out[:, half * 32:(half + 1) * 32, :], in_=kout), r=["score"])
        P.act(lambda e: e.activation(out=sgn[:], in_=iwt[:], func=AF.Sign), r=["iwt"], w=["sgn"])
        P.dve(lambda e: e.tensor_mul(out=absw[:], in0=iwt[:], in1=sgn[:]), r=["iwt", "sgn"], w=["absw"])
        for i in range(NS):
            nk = 8 * (i + 1); nch = 2 * (i + 1); n = nk * 128
            q0 = i * 128
            P.dma(lambda e, i=i: e.dma_start(out=iq[:], in_=iqs[i, :, :]), w=["iq"])
            for h in range(8):
                P.dve(lambda e, h=h, i=i: e.tensor_scalar(out=D[:, h, :], in0=idb[:], scalar1=sgn[:, i * 8 + h:i * 8 + h + 1], scalar2=None, op0=ALU.mult), r=["idb", "sgn"], w=["D"])
            for kc in range(nch):
                for h in range(8):
                    b = (kc * 8 + h) % 3
                    P.pe(lambda e, h=h, kc=kc, b=b: e.matmul(pS[b][:], iq[:, h * 128:(h + 1) * 128], ikt[:, kc * 512:(kc + 1) * 512], start=True, stop=True), r=["iq", "ikt"], w=[f"pS{b}"])
                    col = i * 8 + h
                    if h % 2 == 0:
                        P.act(lambda e, h=h, b=b, col=col: e.activation(out=T[h][:], in_=pS[b][:], func=AF.Relu, scale=absw[:, col:col + 1]), r=[f"pS{b}", "absw"], w=[f"T{h}"])
                    else:
                        P.dve(lambda e, h=h, b=b, col=col: e.tensor_scalar(out=T[h][:], in0=pS[b][:], scalar1=absw[:, col:col + 1], scalar2=0.0, op0=ALU.mult, op1=ALU.max), r=[f"pS{b}", "absw"], w=[f"T{h}"])
                    P.pe(lambda e, h=h: e.matmul(pSc[:], D[:, h, :], T[h][:], start=(h == 0), stop=(h == 7)), r=["D", f"T{h}"], w=["pSc"])
                if kc >= nch - 2:
                    j = kc - (nch - 2)
                    P.dve(lambda e, kc=kc, j=j: e.tensor_tensor(out=score[:, kc * 512:(kc + 1) * 512], in0=pSc[:], in1=cb[:, j * 512:(j + 1) * 512], op=ALU.add), r=["pSc", "cb"], w=["score"])
                else:
                    P.dve(lambda e, kc=kc: e.tensor_copy(out=score[:, kc * 512:(kc + 1) * 512], in_=pSc[:]), r=["pSc"], w=["score"])
            P.dve(lambda e, n=n: e.reduce_max(out=sm[:, 1:2], in_=score[:, 0:n], axis=AX.X), r=["score"], w=["sm"])
            P.dve(lambda e: e.tensor_scalar(out=sm[:, 0:1], in0=sm[:, 1:2], scalar1=-RANGE, scalar2=None, op0=ALU.add), r=["sm"], w=["sm"])
            npc = (n + 4095) // 4096
            for it in range(NITER):
                P.dve(lambda e: e.tensor_tensor(out=sm[:, 2:3], in0=sm[:, 0:1], in1=sm[:, 1:2], op=ALU.add), r=["sm"], w=["sm"])
                P.dve(lambda e: e.tensor_scalar(out=sm[:, 2:3], in0=sm[:, 2:3], scalar1=0.5, scalar2=None, op0=ALU.mult), r=["sm"], w=["sm"])
                P.dve(lambda e: e.memset(cnt4[:], 0.0), w=["cnt4"])
                for pc in range(npc):
                    c0 = pc * 4096; c1 = min(n, c0 + 4096)
                    P.dve(lambda e, c0=c0, c1=c1, pc=pc: e.tensor_scalar(out=junk[:, 0:c1 - c0], in0=score[:, c0:c1], scalar1=sm[:, 2:3], scalar2=0.0, op0=ALU.is_gt, op1=ALU.add, accum_out=cnt4[:, pc:pc + 1]), r=["score", "sm"], w=["junk", "cnt4"])
                P.dve(lambda e, npc=npc: e.reduce_sum(out=sm[:, 3:4], in_=cnt4[:, 0:npc], axis=AX.X), r=["cnt4"], w=["sm"])
                P.dve(lambda e: e.tensor_scalar(out=sm[:, 4:5], in0=sm[:, 3:4], scalar1=255.5, scalar2=None, op0=ALU.is_gt), r=["sm"], w=["sm"])
                P.dve(lambda e: e.tensor_sub(out=sm[:, 5:6], in0=sm[:, 2:3], in1=sm[:, 0:1]), r=["sm"], w=["sm"])
                P.dve(lambda e: e.tensor_sub(out=sm[:, 6:7], in0=sm[:, 1:2], in1=sm[:, 2:3]), r=["sm"], w=["sm"])
                P.dve(lambda e: e.scalar_tensor_tensor(out=sm[:, 0:1], in0=sm[:, 5:6], scalar=sm[:, 4:5], in1=sm[:, 0:1], op0=ALU.mult, op1=ALU.add), r=["sm"], w=["sm"])
                P.dve(lambda e: e.scalar_tensor_tensor(out=sm[:, 1:2], in0=sm[:, 6:7], scalar=sm[:, 4:5], in1=sm[:, 2:3], op0=ALU.mult, op1=ALU.add), r=["sm"], w=["sm"])
            P.pe(lambda e: e.matmul(pO[:], zl[:], zr[:], start=True, stop=False), r=["zl", "zr"], w=["pO"])
            for kc in range(nch):
                if kc % 4 == 0:
                    P.dma(lambda e, kc=kc: e.dma_start(out=ktc[:], in_=KTg[:, :, kc * 512:kc * 512 + 2048]), w=["ktc"])
                    P.dma(lambda e, kc=kc: e.dma_start(out=vc[:], in_=Vg[:, kc * 4:kc * 4 + 16, :]), w=["vc"], q="pool")
                P.dve(lambda e, kc=kc: e.tensor_scalar(out=mbc[:], in0=score[:, kc * 512:(kc + 1) * 512], scalar1=sm[:, 0:1], scalar2=-30000.0, op0=ALU.is_le, op1=ALU.mult), r=["score", "sm"], w=["mbc"])
                for t in range(4):
                    kt = kc * 4 + t
                    lt = (kc % 4) * 4 + t
                    b = kt % 2
                    for h in range(4):
                        po = (h % 2) * 64
                        P.pe(lambda e, h=h, po=po, lt=lt, b=b, q0=q0: e.matmul(pL[b][:, h * 128:(h + 1) * 128], ktc[po:po + 64, h // 2, lt * 128:(lt + 1) * 128], qt[po:po + 64, h // 2, q0:q0 + 128], start=True, stop=False), r=["ktc", "qt"], w=[f"pL{b}"])
                        P.pe(lambda e, h=h, t=t, b=b: e.matmul(pL[b][:, h * 128:(h + 1) * 128], mbc[:, t * 128:(t + 1) * 128], idb[:], start=False, stop=True), r=["mbc", "idb"], w=[f"pL{b}"])
                    P.act(lambda e, b=b: e.activation(out=pT[:], in_=pL[b][:], func=AF.Exp), r=[f"pL{b}"], w=["pT"])
                    for h in range(4):
                        P.pe(lambda e, h=h, lt=lt, kt=kt, nk=nk: e.matmul(pO[:, h * 128:h * 128 + 65], pT[:, h * 128:(h + 1) * 128], vc[:, lt, h * 80:h * 80 + 65], start=False, stop=(kt == nk - 1)), r=["pT", "vc"], w=["pO"])
            pv = pO[:].rearrange("p (h e) -> p h e", e=128)
            P.dve(lambda e, pv=pv: e.reciprocal(out=rc[:], in_=pv[:, :, 64]), r=["pO"], w=["rc"])
            P.dve(lambda e, pv=pv: e.tensor_tensor(out=ost[:], in0=pv[:, :, 0:64], in1=rc[:].unsqueeze(2).to_broadcast([128, 4, 64]), op=ALU.mult), r=["pO", "rc"], w=["ost"])
            P.dma(lambda e, q0=q0: e.dma_start(out=o_sa[q0:q0 + 128, :], in_=ost[:].rearrange("p h d -> p (h d)")), r=["ost"])
        P.emit()
    return nc

EPS = 1e-6
NT = 2048
NG = 4


def build_C():
    nc = bass.Bass("TRN2", target_bir_lowering=False)
    def din(name, shape, dt=F32): return nc.dram_tensor(name, shape, dt, kind="ExternalInput").ap()
    def dout(name, shape, dt=F32): return nc.dram_tensor(name, shape, dt, kind="ExternalOutput").ap()
    x = din("x", [NT, 1024]); o_intra = din("o_intra", [NG, 4, 64, 1024]); qhat = din("qhat", [4, 128, NT], BF16)
    Sp = din("Sp", [NG, 128, 4096]); g_in = din("g_in", [NT, 512]); omem = din("omem", [NT, 256]); o_sa = din("o_sa", [NT, 256])
    w_out = din("w_out", [128, 8192]); w_fi = din("w_fi", [44, 128, 1024]); w_fo = din("w_fo", [128, 22 * 1024])
    gffn = din("gffn", [128, 1024]); ghg = din("ghg", [64, 512]); ident = din("ident", [128, 128])
    x_out = dout("x_out", [NT, 1024])
    P = Prog(nc)
    with contextlib.ExitStack() as st:
        def sb(name, shape, dt=F32): return st.enter_context(nc.sbuf_tensor(name, shape, dt))
        def ps(name, shape, dt=F32): return st.enter_context(nc.psum_tensor(name, shape, dt))
        wout = sb("wout", [128, 8, 1024], BF16); wfo = sb("wfo", [128, 22, 1024], BF16)
        wt = [sb(f"wt{i}", [128, 8, 128], BF16) for i in range(4)]
        gf = sb("gf", [128, 1024]); gh = sb("gh", [64, 4, 128]); idb = sb("idb", [128, 128], BF16)
        uT = sb("uT", [128, 22, 512], BF16)
        xg = sb("xg", [128, 4, 1024])
        h2T = sb("h2T", [128, 8, 512], BF16); mixT = sb("mixT", [128, 8, 512], BF16)
        qh = sb("qh", [128, 4, 512], BF16); spg = sb("spg", [128, 4, 8, 128], BF16)
        oig = sb("oig", [64, 4, 8, 128])
        o32 = sb("o32", [64, 4, 128]); gch = sb("gch", [64, 512]); mixb = sb("mixb", [64, 1024], BF16)
        junk = sb("junk", [128, 1024], BF16); epsb = sb("epsb", [128, 1]); P.dve(lambda e: e.memset(epsb[:], EPS), w=["epsb"]); ss = sb("ss", [128, 8])
        hb = sb("hb", [128, 1024], BF16); sa = sb("sa", [128, 512])
        ptr = ps("ptr", [128, 1024], BF16)
        pf = ps("pf", [128, 512]); pw = ps("pw", [128, 512]); pt = ps("pt", [128, 512]); po = ps("po", [128, 512])
        for q4 in range(4):
            P.dma(lambda e, q4=q4: e.dma_start(out=wout[:, q4 * 2:(q4 + 1) * 2, :], in_=w_out[:, q4 * 2048:(q4 + 1) * 2048]), w=["wout"], q="pool")
        for q11 in range(11):
            P.dma(lambda e, q11=q11: e.dma_start(out=wfo[:, q11 * 2:(q11 + 1) * 2, :], in_=w_fo[:, q11 * 2048:(q11 + 1) * 2048]), w=["wfo"], q="pool")
        P.dma(lambda e: e.dma_start(out=gf[:], in_=gffn[:, :]), w=["gf"])
        P.dma(lambda e: e.dma_start(out=gh[:], in_=ghg[:, :]), w=["gh"])
        P.dma(lambda e: e.dma_start(out=idb[:], in_=ident[:, :]), w=["idb"], q="pool")
        wti = 0
        for g in range(NG):
            T0 = g * 512
            for h in range(4):
                P.dma(lambda e, h=h, T0=T0: e.dma_start(out=qh[:, h, :], in_=qhat[h, :, T0:T0 + 512]), w=["qh"])
            P.dma(lambda e, g=g: e.dma_start(out=spg[:], in_=Sp[g, :, :]), w=["spg"], q="pool")
            P.dma(lambda e, g=g: e.dma_start(out=oig[:], in_=o_intra[g, :, :, :].rearrange("h p f -> p h f")), w=["oig"])
            for c in range(8):
                r0 = T0 + c * 64
                for h in range(4):
                    P.pe(lambda e, h=h, c=c: e.matmul(pw[0:64, h * 128:(h + 1) * 128], qh[:, h, c * 64:(c + 1) * 64], spg[:, h, c, :], start=True, stop=True), r=["qh", "spg"], w=["pw"])
                P.dve(lambda e, c=c: e.tensor_tensor(out=o32[:], in0=pw[0:64, :].rearrange("p (h v) -> p h v", h=4), in1=oig[:, :, c, :], op=ALU.add), r=["pw", "oig"], w=["o32"])
                for h in range(4):
                    P.act(lambda e, h=h: e.activation(out=junk[0:64, 0:128], in_=o32[:, h, :], func=AF.Square, accum_out=ss[0:64, h:h + 1]), r=["o32"], w=["junk", "ss"])
                P.act(lambda e: e.activation(out=ss[0:64, 4:8], in_=ss[0:64, 0:4], func=AF.Sqrt, bias=epsb[0:64, 0:1], scale=1.0 / 128), r=["ss", "epsb"], w=["ss"])
                P.dve(lambda e: e.reciprocal(out=ss[0:64, 4:8], in_=ss[0:64, 4:8]), r=["ss"], w=["ss"])
                P.dma(lambda e, r0=r0: e.dma_start(out=gch[:], in_=g_in[r0:r0 + 64, :]), w=["gch"])
                P.act(lambda e: e.activation(out=gch[:], in_=gch[:], func=AF.Silu), r=["gch"], w=["gch"])
                P.dve(lambda e: e.tensor_tensor(out=o32[:], in0=o32[:], in1=ss[0:64, 4:8].unsqueeze(2).to_broadcast([64, 4, 128]), op=ALU.mult), r=["o32", "ss"], w=["o32"])
                P.dve(lambda e: e.tensor_tensor(out=o32[:], in0=o32[:], in1=gh[:], op=ALU.mult), r=["o32", "gh"], w=["o32"])
                P.dma(lambda e, r0=r0: e.dma_start(out=mixb[:, 512:768], in_=o_sa[r0:r0 + 64, :]), w=["mixb"], q="pool")
                P.dma(lambda e, r0=r0: e.dma_start(out=mixb[:, 768:1024], in_=omem[r0:r0 + 64, :]), w=["mixb"], q="pool")
                P.dve(lambda e: e.tensor_tensor(out=mixb[:, 0:512], in0=o32[:].rearrange("p h v -> p (h v)"), in1=gch[:], op=ALU.mult), r=["o32", "gch"], w=["mixb"])
                for kc in range(8):
                    P.pe(lambda e, kc=kc: e.transpose(ptr[:, kc * 64:(kc + 1) * 64], mixb[:, kc * 128:(kc + 1) * 128], idb[0:64, 0:64]), r=["mixb", "idb"], w=["ptr"])
                P.act(lambda e, c=c: e.copy(out=mixT[:, :, c * 64:(c + 1) * 64], in_=ptr[:, 0:512].rearrange("p (k t) -> p k t", k=8)), r=["ptr"], w=["mixT"])
            for blk in range(4):
                r0 = T0 + blk * 128
                P.dma(lambda e, r0=r0, blk=blk: e.dma_start(out=xg[:, blk, :], in_=x[r0:r0 + 128, :]), w=[f"xg{blk}"])
                for half in range(2):
                    for kc in range(8):
                        P.pe(lambda e, kc=kc, half=half, blk=blk: e.matmul(pt[:], mixT[:, kc, blk * 128:(blk + 1) * 128], wout[:, kc, half * 512:(half + 1) * 512], start=(kc == 0), stop=(kc == 7)), r=["mixT", "wout"], w=["pt"])
                    P.dve(lambda e, half=half, blk=blk: e.tensor_tensor(out=xg[:, blk, half * 512:(half + 1) * 512], in0=xg[:, blk, half * 512:(half + 1) * 512], in1=pt[:], op=ALU.add), r=["pt", f"xg{blk}"], w=[f"xg{blk}"])
                P.act(lambda e, blk=blk: e.activation(out=junk[:], in_=xg[:, blk, :], func=AF.Square, accum_out=ss[:, 0:1]), r=[f"xg{blk}"], w=["junk", "ss"])
                P.act(lambda e: e.activation(out=ss[:, 1:2], in_=ss[:, 0:1], func=AF.Sqrt, bias=epsb[:, 0:1], scale=1.0 / 1024), r=["ss", "epsb"], w=["ss"])
                P.dve(lambda e: e.reciprocal(out=ss[:, 1:2], in_=ss[:, 1:2]), r=["ss"], w=["ss"])
                P.dve(lambda e, blk=blk: e.scalar_tensor_tensor(out=hb[:], in0=xg[:, blk, :], scalar=ss[:, 1:2], in1=gf[:], op0=ALU.mult, op1=ALU.mult), r=[f"xg{blk}", "ss", "gf"], w=["hb"])
                for kc in range(8):
                    P.pe(lambda e, kc=kc: e.transpose(ptr[:, kc * 128:(kc + 1) * 128], hb[:, kc * 128:(kc + 1) * 128], idb[:]), r=["hb", "idb"], w=["ptr"])
                P.act(lambda e, blk=blk: e.copy(out=h2T[:, :, blk * 128:(blk + 1) * 128], in_=ptr[:].rearrange("p (k t) -> p k t", k=8)), r=["ptr"], w=["h2T"])
            for j in range(22):
                ia = wti % 4; ib = (wti + 1) % 4; wti += 2
                P.dma(lambda e, j=j, ia=ia: e.dma_start(out=wt[ia][:], in_=w_fi[j, :, :]), w=[f"wt{ia}"], q="pool")
                P.dma(lambda e, j=j, ib=ib: e.dma_start(out=wt[ib][:], in_=w_fi[22 + j, :, :]), w=[f"wt{ib}"], q="pool")
                for kc in range(8):
                    P.pe(lambda e, kc=kc, ia=ia: e.matmul(pf[:], wt[ia][:, kc, :], h2T[:, kc, :], start=(kc == 0), stop=(kc == 7)), r=[f"wt{ia}", "h2T"], w=["pf"])
                for kc in range(8):
                    P.pe(lambda e, kc=kc, ib=ib: e.matmul(po[:], wt[ib][:, kc, :], h2T[:, kc, :], start=(kc == 0), stop=(kc == 7)), r=[f"wt{ib}", "h2T"], w=["po"])
                P.act(lambda e: e.activation(out=sa[:], in_=pf[:], func=AF.Silu), r=["pf"], w=["sa"])
                P.dve(lambda e, j=j: e.tensor_tensor(out=uT[:, j, :], in0=sa[:], in1=po[:], op=ALU.mult), r=["sa", "po"], w=["uT"])
            for blk in range(4):
                r0 = T0 + blk * 128
                for half in range(2):
                    for j in range(22):
                        P.pe(lambda e, j=j, half=half, blk=blk: e.matmul(pt[:], uT[:, j, blk * 128:(blk + 1) * 128], wfo[:, j, half * 512:(half + 1) * 512], start=(j == 0), stop=(j == 21)), r=["uT", "wfo"], w=["pt"])
                    P.dve(lambda e, half=half, blk=blk: e.tensor_tensor(out=xg[:, blk, half * 512:(half + 1) * 512], in0=xg[:, blk, half * 512:(half + 1) * 512], in1=pt[:], op=ALU.add), r=["pt", f"xg{blk}"], w=[f"xg{blk}"])
                P.dma(lambda e, r0=r0, blk=blk: e.dma_start(out=x_out[r0:r0 + 128, :], in_=xg[:, blk, :]), r=[f"xg{blk}"])
        P.emit()
    return nc


CORES = list(range(8))


def _shard_tok(a, c):
    return np.ascontiguousarray(a.reshape(16, 8, 128, -1)[:, c].reshape(2048, -1))


def _c(a):
    return np.ascontiguousarray(a)


def make_CBs():
    CBs = []
    for c in range(8):
        cb = np.zeros((128, 8, 128), np.float32)
        cb[:, c + 1:, :] = -1e30
        tri = np.where(np.arange(128)[None, :] <= np.arange(128)[:, None], 0.0, -1e30).astype(np.float32)
        cb[:, c, :] = tri
        CBs.append(_c(cb.reshape(128, 1024)))
    return CBs


def make_A_inputs(inp, xs, layer, cA):
    base = dict(cA)
    base["w_in"] = _c(inp["w_in"][layer]); base["gmix"] = _c(np.broadcast_to(inp["norm_mix"][layer], (128, 1024)))
    base["mem"] = _c(inp["mem"][0]); base["gmem"] = _c(np.broadcast_to(inp["norm_mem"][layer], (128, 1024)))
    base["w_kv"] = _c(inp["w_mem_kv"][layer])
    base["lbl"] = _c(inp["lb_logits"].reshape(2, 4, 128).transpose(2, 0, 1).reshape(128, 8))
    base["lsel"] = np.full((128, 1), float(layer), np.float32)
    base["gcols"] = _c(np.stack([np.tile(inp["sa_q_gain"][layer], 2), np.tile(inp["sa_k_gain"][layer], 2),
                                 np.tile(inp["mem_q_gain"][layer], 2), np.tile(inp["mem_k_gain"][layer], 2)], axis=1))
    return [dict(base, x=xs[c]) for c in CORES]


def make_B_inputs(ra, CBs, ident):
    KTg = np.stack([np.asarray(ra[c]["KT"]).reshape(2, 128, 16, 128) for c in CORES], axis=3).reshape(2, 128, 16384)
    KTg = _c(KTg.transpose(1, 0, 2))
    Vg = np.stack([np.asarray(ra[c]["V"]).reshape(16, 128, 320) for c in CORES], axis=1).reshape(128, 128, 320)
    Vg = _c(Vg.transpose(1, 0, 2))
    ikg = _c(np.stack([np.asarray(ra[c]["ikT"]).reshape(64, 16, 128) for c in CORES], axis=2).reshape(64, 16384))
    a_glob = np.stack([np.asarray(ra[c]["a_out"]).reshape(128, 4, 4, 4, 2).transpose(0, 2, 1, 3, 4).reshape(128, 4, 16, 2) for c in CORES], axis=3)
    a_glob = a_glob.reshape(128, 4, 256).transpose(1, 0, 2)
    kv_glob = np.stack([np.asarray(ra[c]["kv_out"]).reshape(4, 4, 128, 4, 2, 128).transpose(1, 2, 0, 3, 4, 5).reshape(4, 128, 16, 2, 128) for c in CORES], axis=3)
    kv_glob = kv_glob.reshape(4, 128, 256, 128)
    maps = []
    for c in CORES:
        h, half = c // 2, c % 2
        d = dict(ident=ident, KTg=KTg, Vg=Vg, ikT=ikg, CB=CBs[c])
        d["QT"] = _c(np.asarray(ra[c]["QT"]).transpose(1, 0, 2))
        d["iqs"] = _c(np.asarray(ra[c]["iqT"]).reshape(64, 8, 16, 128).transpose(2, 0, 1, 3).reshape(16, 64, 1024))
        d["iwp"] = _c(np.asarray(ra[c]["iw_out"])[:, :8].reshape(16, 128, 8).transpose(1, 0, 2).reshape(128, 128))
        d["a_s"] = _c(a_glob[h])
        d["kv_s"] = _c(kv_glob[h][:, :, half * 64:(half + 1) * 64].transpose(0, 2, 1))
        maps.append(d)
    return maps


def make_C_inputs(inp, layer, xs, ra, rb, ident):
    Sg = np.stack([np.concatenate([np.asarray(rb[2 * h]["S_out"]), np.asarray(rb[2 * h + 1]["S_out"])], axis=1) for h in range(4)], axis=0)
    Sprev = np.concatenate([np.zeros_like(Sg[..., :1]), Sg[..., :-1]], axis=-1)
    Sr = Sprev.reshape(4, 128, 128, 16, 8, 2)
    wo = _c(inp["w_out"][layer].reshape(8, 128, 1024).transpose(1, 0, 2).reshape(128, 8192))
    wfi = _c(inp["w_ffn_in"][layer].reshape(8, 128, 44, 128).transpose(2, 1, 0, 3).reshape(44, 128, 1024))
    wfo = _c(inp["w_ffn_out"][layer].reshape(22, 128, 1024).transpose(1, 0, 2).reshape(128, 22 * 1024))
    gffn = _c(np.broadcast_to(inp["norm_ffn"][layer], (128, 1024)))
    ghg = _c(np.broadcast_to(np.tile(inp["hg_out_gain"][layer], 4), (64, 512)))
    maps = []
    for c in CORES:
        sp = Sr[:, :, :, :, c, :].reshape(4, 128, 128, 4, 4, 2).transpose(3, 1, 0, 4, 5, 2).reshape(4, 128, 4096)
        d = dict(x=xs[c], o_intra=np.asarray(ra[c]["o_intra"]), qhat=np.asarray(ra[c]["qhat"]), Sp=_c(sp),
                 g_in=np.asarray(ra[c]["g_out"]), omem=np.asarray(ra[c]["omem"]), o_sa=np.asarray(rb[c]["o_sa"]),
                 w_out=wo, w_fi=wfi, w_fo=wfo, gffn=gffn, ghg=ghg, ident=ident)
        maps.append(d)
    return maps


def kernel(**inp):
    inp = {k: np.asarray(v) for k, v in inp.items()}
    x = inp["x"][0]
    xs = [_shard_tok(x, c) for c in CORES]
    cA = consts_A()
    ident = cA["ident"]
    ncA = build_A(); ncB = build_B(); ncC = build_C()
    CBs = make_CBs()
    for layer in range(2):
        ra = run_bass_kernel_spmd(ncA, make_A_inputs(inp, xs, layer, cA), core_ids=CORES).results
        rb = run_bass_kernel_spmd(ncB, make_B_inputs(ra, CBs, ident), core_ids=CORES).results
        rc_ = run_bass_kernel_spmd(ncC, make_C_inputs(inp, layer, xs, ra, rb, ident), core_ids=CORES).results
        xs = [np.asarray(rc_[c]["x_out"]) for c in CORES]
    out = np.stack([xs[c].reshape(16, 128, 1024) for c in CORES], axis=1).reshape(1, 16384, 1024)
    return out.astype(np.float32)
```

```python
import contextlib
import numpy as np
import concourse.bass as bass
import concourse.mybir as mybir
from concourse.bass_utils import run_bass_kernel_spmd

F32 = mybir.dt.float32
BF16 = mybir.dt.bfloat16
AF = mybir.ActivationFunctionType
ALU = mybir.AluOpType
AX = mybir.AxisListType

ENGS = ("pe", "act", "dve", "pool", "sp")


class Prog:
    def __init__(self, nc):
        self.nc = nc
        self.ops = []
        self.same_engine_sync = True
        self.max_outstanding = 3

    def op(self, eng, fn, reads=(), writes=(), dma=False):
        self.ops.append((eng, fn, tuple(reads), tuple(writes), dma))

    def pe(self, fn, r=(), w=()): self.op("pe", fn, r, w)
    def act(self, fn, r=(), w=()): self.op("act", fn, r, w)
    def dve(self, fn, r=(), w=()): self.op("dve", fn, r, w)
    def pool(self, fn, r=(), w=()): self.op("pool", fn, r, w)
    def dma(self, fn, r=(), w=(), q="sp"): self.op(q, fn, r, w, True)

    def emit(self):
        nc = self.nc
        ops = self.ops
        n = len(ops)
        R = self.max_outstanding
        stream = []
        prev_same = [None] * n if False else None
        qcount = {}
        for (e, _, _, _, d) in ops:
            if d:
                k = qcount.get(e, 0); qcount[e] = k + 1
                stream.append("%s_dma%d" % (e, k % R))
            else:
                stream.append(e)
        last_w = {}
        readers = {}
        deps = [set() for _ in range(n)]
        for i, (e, fn, rs, ws, d) in enumerate(ops):
            for k in rs:
                if k in last_w:
                    deps[i].add(last_w[k])
            for k in ws:
                if k in last_w:
                    deps[i].add(last_w[k])
                for j in readers.get(k, ()):
                    if j != i:
                        deps[i].add(j)
            for k in rs:
                readers.setdefault(k, []).append(i)
            for k in ws:
                last_w[k] = i
                readers[k] = []
        needed = [False] * n
        for i in range(n):
            keep = set()
            for j in deps[i]:
                if stream[j] == stream[i] == "pe":
                    continue
                if (not self.same_engine_sync) and stream[j] == stream[i] and "_dma" not in stream[i]:
                    continue
                keep.add(j)
            deps[i] = keep
        for i in range(n):
            for j in deps[i]:
                needed[j] = True
        for i in range(n):
            if ops[i][4]:
                needed[i] = True
        cnt = {}
        val = [0] * n
        for i in range(n):
            s = stream[i]
            if needed[i]:
                cnt[s] = cnt.get(s, 0) + (16 if "_dma" in s else 1)
            val[i] = cnt.get(s, 0)
        streams = sorted(set(stream))
        self.final_counts = dict(cnt)
        import contextlib
        with contextlib.ExitStack() as st:
            sems = {s: st.enter_context(nc.semaphore("sem_" + s)) for s in streams}
            block = st.enter_context(nc.Block())
            per_eng = {e: [i for i in range(n) if ops[i][0] == e] for e in ENGS}

            def run(engname, eng):
                waited = {}
                ndma = 0
                for i in per_eng[engname]:
                    if ops[i][4]:
                        s_ = stream[i]
                        lim = val[i] - 16
                        if lim > 0 and waited.get(s_, 0) < lim:
                            eng.wait_ge(sems[s_], lim)
                            waited[s_] = lim
                        ndma += 1
                    need = {}
                    for j in deps[i]:
                        s = stream[j]
                        need[s] = max(need.get(s, 0), val[j])
                    for s, v in need.items():
                        if waited.get(s, 0) < v:
                            eng.wait_ge(sems[s], v)
                            waited[s] = v
                    ins = ops[i][1](eng)
                    if needed[i]:
                        ins.then_inc(sems[stream[i]], 16 if ops[i][4] else 1)
                if engname == "sp":
                    for s in streams:
                        if "_dma" in s and cnt.get(s, 0) > 0:
                            eng.wait_ge(sems[s], cnt[s])

            @block.sync
            def _(e): run("sp", e)

            @block.gpsimd
            def _(e): run("pool", e)

            @block.scalar
            def _(e): run("act", e)

            @block.vector
            def _(e): run("dve", e)

            @block.tensor
            def _(e): run("pe", e)

EPS = 1e-6
NT = 2048
NG = 4
IN_W = 3656
C_HQ, C_HF, C_HI, C_HG, C_SQ, C_SK, C_SV, C_IQ, C_IK, C_IW, C_MQ = 0, 512, 1024, 1536, 2048, 2304, 2560, 2816, 3328, 3392, 3400


class _Stop(Exception): pass

def build_A(STOP=99.0):
    nc = bass.Bass("TRN2", target_bir_lowering=False)
    def din(name, shape, dt=F32): return nc.dram_tensor(name, shape, dt, kind="ExternalInput").ap()
    def dout(name, shape, dt=F32): return nc.dram_tensor(name, shape, dt, kind="ExternalOutput").ap()
    x = din("x", [NT, 1024]); w_in = din("w_in", [1024, IN_W]); gmix = din("gmix", [128, 1024])
    mem = din("mem", [256, 1024]); gmem = din("gmem", [128, 1024]); w_kv = din("w_kv", [1024, 512])
    lbl = din("lbl", [128, 8]); lsel = din("lsel", [128, 1]); gcols = din("gcols", [128, 4])
    ident = din("ident", [128, 128]); bdin = din("bd", [128, 128]); cmaskin = din("cmask", [64, 512]); rmaskin = din("rmask", [128, 512])
    o_intra = dout("o_intra", [NG, 4, 64, 1024]); qhat = dout("qhat", [4, 128, NT], BF16); g_out = dout("g_out", [NT, 512])
    kv_out = dout("kv_out", [NG, 4, 128, 1024]); a_out = dout("a_out", [128, NG * 4 * 8]); omem = dout("omem", [NT, 256])
    QT = dout("QT", [2, 128, NT], BF16); KT = dout("KT", [2, 128, NT], BF16); V = dout("V", [NT, 320], BF16)
    ikT = dout("ikT", [64, NT], BF16); iqT = dout("iqT", [64, 8, NT], BF16); iw_out = dout("iw_out", [NT, 128])
    P = Prog(nc)
    def chk(k):
        if k > STOP: raise _Stop()
    with contextlib.ExitStack() as st:
      try:
          def sb(name, shape, dt=F32): return st.enter_context(nc.sbuf_tensor(name, shape, dt))
          def ps(name, shape, dt=F32): return st.enter_context(nc.psum_tensor(name, shape, dt))
          wbf = sb("wbf", [128, 8, IN_W], BF16)
          gainb = sb("gainb", [128, 1024]); gmemb = sb("gmemb", [128, 1024])
          idb = sb("idb", [128, 128], BF16); bd = sb("bdb", [128, 128], BF16); cmask = sb("cmask_s", [64, 512], BF16); rmask = sb("rmask_s", [128, 512])
          lbt = sb("lbt", [128, 8]); lselt = sb("lselt", [128, 1]); lb = sb("lb", [128, 4]); oml = sb("oml", [128, 4]); gc = sb("gc", [128, 4])
          xin = [sb(f"xin{i}", [128, 1024]) for i in range(2)]
          hb = [sb(f"hb{i}", [128, 1024], BF16) for i in range(2)]
          junk = sb("junk", [128, 1024], BF16); epsb = sb("epsb", [128, 1]); P.dve(lambda e: e.memset(epsb[:], EPS), w=["epsb"])
          ss = sb("ss", [128, 4])
          hT = sb("hT", [128, 8, 512], BF16)
          ptr = ps("ptr", [128, 1024], BF16)
          pf = ps("pf", [128, 512]); pt = ps("pt", [128, 512]); pw = ps("pw", [128, 512])
          pbA = ps("pbA", [128, 1024]); pbB = ps("pbB", [128, 1024])
          for kc in range(8):
              P.dma(lambda e, kc=kc: e.dma_start(out=wbf[:, kc, :], in_=w_in[kc * 128:(kc + 1) * 128, :]), w=["wbf"], q="pool")
          P.dma(lambda e: e.dma_start(out=gainb[:], in_=gmix[:, :]), w=["gainb"])
          P.dma(lambda e: e.dma_start(out=gmemb[:], in_=gmem[:, :]), w=["gmemb"])
          P.dma(lambda e: e.dma_start(out=idb[:], in_=ident[:, :]), w=["idb"], q="pool")
          P.dma(lambda e: e.dma_start(out=bd[:], in_=bdin[:, :]), w=["bd"], q="pool")
          P.dma(lambda e: e.dma_start(out=cmask[:], in_=cmaskin[:, :]), w=["cmask"], q="pool")
          P.dma(lambda e: e.dma_start(out=rmask[:], in_=rmaskin[:, :]), w=["rmask"])
          P.dma(lambda e: e.dma_start(out=lbt[:], in_=lbl[:, :]), w=["lbt"])
          P.dma(lambda e: e.dma_start(out=lselt[:], in_=lsel[:, :]), w=["lselt"])
          P.dma(lambda e: e.dma_start(out=gc[:], in_=gcols[:, :]), w=["gc"])
          P.dve(lambda e: e.tensor_sub(out=lb[:], in0=lbt[:, 4:8], in1=lbt[:, 0:4]), r=["lbt"], w=["lb"])
          P.act(lambda e: e.activation(out=lb[:], in_=lb[:], func=AF.Sigmoid), r=["lb"], w=["lb"])
          P.dve(lambda e: e.tensor_scalar(out=lb[:], in0=lb[:], scalar1=lselt[:, 0:1], scalar2=None, op0=ALU.mult), r=["lb", "lselt"], w=["lb"])
          P.dve(lambda e: e.tensor_scalar(out=oml[:], in0=lb[:], scalar1=-1.0, scalar2=1.0, op0=ALU.mult, op1=ALU.add), r=["lb"], w=["oml"])
          P.dve(lambda e: e.tensor_scalar(out=gc[:, 0:1], in0=gc[:, 0:1], scalar1=0.125, scalar2=None, op0=ALU.mult), r=["gc"], w=["gc"])
          P.dve(lambda e: e.tensor_scalar(out=gc[:, 2:3], in0=gc[:, 2:3], scalar1=0.125, scalar2=None, op0=ALU.mult), r=["gc"], w=["gc"])

          def rms_to_hb(src, srck, gtile, gk, dst, dstk):
              P.act(lambda e: e.activation(out=junk[:], in_=src, func=AF.Square, accum_out=ss[:, 0:1]), r=[srck], w=["junk", "ss"])
              P.act(lambda e: e.activation(out=ss[:, 1:2], in_=ss[:, 0:1], func=AF.Sqrt, bias=epsb[:, 0:1], scale=1.0 / 1024), r=["ss", "epsb"], w=["ss"])
              P.dve(lambda e: e.reciprocal(out=ss[:, 1:2], in_=ss[:, 1:2]), r=["ss"], w=["ss"])
              P.dve(lambda e: e.scalar_tensor_tensor(out=dst, in0=src, scalar=ss[:, 1:2], in1=gtile, op0=ALU.mult, op1=ALU.mult), r=[srck, "ss", gk], w=[dstk])

          def transpose8(srcb, srck, dst3, dstk):
              for kc in range(8):
                  P.pe(lambda e, kc=kc: e.transpose(ptr[:, kc * 128:(kc + 1) * 128], srcb[:, kc * 128:(kc + 1) * 128], idb[:]), r=[srck, "idb"], w=["ptr"])
              P.act(lambda e: e.copy(out=dst3, in_=ptr[:].rearrange("p (k t) -> p k t", k=8)), r=["ptr"], w=[dstk])

          def head_norm(psrc, psk, gcol, dst, dstk, tmpa, tmpb, n=512):
              P.act(lambda e: e.activation(out=tmpa, in_=psrc, func=AF.Square), r=[psk], w=["hn_a"])
              P.pe(lambda e: e.matmul(pw[:, 0:n], bd[:], tmpa, start=True, stop=True), r=["hn_a", "bd"], w=["pw"])
              P.act(lambda e: e.activation(out=tmpb, in_=pw[:, 0:n], func=AF.Sqrt, bias=epsb[:, 0:1], scale=1.0), r=["pw", "epsb"], w=["hn_b"])
              P.dve(lambda e: e.reciprocal(out=tmpb, in_=tmpb), r=["hn_b"], w=["hn_b"])
              P.dve(lambda e: e.scalar_tensor_tensor(out=dst, in0=psrc, scalar=gc[:, gcol:gcol + 1], in1=tmpb, op0=ALU.mult, op1=ALU.mult), r=[psk, "hn_b", "gc"], w=[dstk])

          hn_a = sb("hn_a", [128, 512], BF16); hn_b = sb("hn_b", [128, 512])
          chk(1)
          wkv = sb("wkv", [128, 8, 512], BF16)
          memT = sb("memT", [128, 8, 256], BF16)
          kmT = sb("kmT", [128, 2, 256], BF16)
          vm1 = sb("vm1", [128, 2, 4, 80], BF16)
          for kc in range(8):
              P.dma(lambda e, kc=kc: e.dma_start(out=wkv[:, kc, :], in_=w_kv[kc * 128:(kc + 1) * 128, :]), w=["wkv"], q="pool")
          for mt in range(2):
              P.dma(lambda e, mt=mt: e.dma_start(out=xin[mt][:], in_=mem[mt * 128:(mt + 1) * 128, :]), w=[f"xin{mt}"])
              rms_to_hb(xin[mt][:], f"xin{mt}", gmemb[:], "gmemb", hb[mt][:], f"hb{mt}")
              transpose8(hb[mt], f"hb{mt}", memT[:, :, mt * 128:(mt + 1) * 128], "memT")
          for hp in range(2):
              for kc in range(8):
                  P.pe(lambda e, kc=kc, hp=hp: e.matmul(pf[:, 0:256], wkv[:, kc, hp * 128:(hp + 1) * 128], memT[:, kc, :], start=(kc == 0), stop=(kc == 7)), r=["wkv", "memT"], w=["pf"])
              head_norm(pf[:, 0:256], "pf", 3, kmT[:, hp, :], "kmT", hn_a[:, 0:256], hn_b[:, 0:256], n=256)
          P.dve(lambda e: e.memset(vm1[:], 1.0), w=["vm1"])
          for mt in range(2):
              for kc in range(8):
                  P.pe(lambda e, kc=kc, mt=mt: e.matmul(pt[:, 0:256], memT[:, kc, mt * 128:(mt + 1) * 128], wkv[:, kc, 256:512], start=(kc == 0), stop=(kc == 7)), r=["wkv", "memT"], w=["pt"])
              P.act(lambda e, mt=mt: e.copy(out=vm1[:, mt, :, 0:64], in_=pt[:, 0:256].rearrange("p (h d) -> p h d", h=4)), r=["pt"], w=["vm1"])

          chk(2)
          def w32(name): return sb(name, [128, 512])
          def w16(name): return sb(name, [128, 512], BF16)
          sig = w32("sig"); fg = w32("fg"); lf = w32("lf"); bcum = w32("bcum"); qs = w32("qs"); kk = w32("kk")
          e1 = w32("e1"); e2 = w32("e2"); eb = w32("eb")
          qt = w16("qt"); kt = w16("kt"); qh = w16("qh")
          aall = sb("aall", [128, NG * 4 * 8])
          sm = sb("sm", [128, 5, 8])
          AT = sb("AT", [64, 512], BF16); ktok = sb("ktok", [64, 8, 128], BF16)
          vtok = sb("vtok", [64, 8, 512], BF16)
          oist = sb("oist", [64, 8, 128]); kvst = sb("kvst", [128, 8, 128])
          gst = sb("gst", [128, 512]); vst = sb("vst", [128, 4, 80], BF16); iwst = sb("iwst", [128, 128])
          qn = w16("qn"); mqT = sb("mqT", [128, 2, 512], BF16)
          iqst = sb("iqst", [64, 8, 512], BF16); ikst = sb("ikst", [64, 512], BF16)
          PT = sb("PT", [128, 2, 4, 512], BF16)
          rc = sb("rc", [128, 4]); omst = sb("omst", [128, 4, 64])
          P.dve(lambda e: e.memset(vst[:], 1.0), w=["vst"])
          P.dve(lambda e: e.memset(iwst[:], 0.0), w=["iwst"])

          def proj_f(col0, ncols, bank=pf, bk="pf"):
              for kc in range(8):
                  P.pe(lambda e, kc=kc: e.matmul(bank[0:ncols, :], wbf[:, kc, col0:col0 + ncols], hT[:, kc, :], start=(kc == 0), stop=(kc == 7)), r=["wbf", "hT"], w=[bk])

          def proj_t(t0, m, col0, ncols, out_ap, bk):
              for kc in range(8):
                  P.pe(lambda e, kc=kc: e.matmul(out_ap, hT[:, kc, t0:t0 + m], wbf[:, kc, col0:col0 + ncols], start=(kc == 0), stop=(kc == 7)), r=["wbf", "hT"], w=[bk])

          for g in range(NG):
              T0 = g * 512
              for blk in range(4):
                  xi = blk % 2
                  P.dma(lambda e, xi=xi, blk=blk, T0=T0: e.dma_start(out=xin[xi][:], in_=x[T0 + blk * 128:T0 + (blk + 1) * 128, :]), w=[f"xin{xi}"])
                  rms_to_hb(xin[xi][:], f"xin{xi}", gainb[:], "gainb", hb[xi][:], f"hb{xi}")
                  transpose8(hb[xi], f"hb{xi}", hT[:, :, blk * 128:(blk + 1) * 128], "hT")
              chk(3)
              for c in range(8):
                  proj_t(c * 64, 64, C_HI, 512, pt[0:64, :], "pt")
                  P.act(lambda e, c=c: e.copy(out=vtok[:, c, :], in_=pt[0:64, :]), r=["pt"], w=["vtok"])
              chk(4)
              import os
              for h in range(0 if os.environ.get('SKIP4') else 4):
                  proj_f(C_HF + h * 128, 128)
                  P.act(lambda e: e.activation(out=sig[:], in_=pf[:], func=AF.Sigmoid), r=["pf"], w=["sig"])
                  P.dve(lambda e, h=h: e.tensor_scalar(out=fg[:], in0=sig[:], scalar1=oml[:, h:h + 1], scalar2=lb[:, h:h + 1], op0=ALU.mult, op1=ALU.add), r=["sig", "oml", "lb"], w=["fg"])
                  P.act(lambda e: e.activation(out=lf[:], in_=fg[:], func=AF.Ln), r=["fg"], w=["lf"])
                  P.dve(lambda e: e.tensor_scalar(out=kk[:], in0=fg[:], scalar1=-1.0, scalar2=1.0, op0=ALU.mult, op1=ALU.add), r=["fg"], w=["kk"])
                  proj_f(C_HQ + h * 128, 128)
                  P.act(lambda e: e.activation(out=qs[:], in_=pf[:], func=AF.Silu), r=["pf"], w=["qs"])
                  P.dve(lambda e: e.tensor_tensor_scan(out=bcum[:], data0=rmask[:], data1=lf[:], initial=0.0, op0=ALU.mult, op1=ALU.add), r=["rmask", "lf"], w=["bcum"])
                  b3 = bcum[:].rearrange("p (c t) -> p c t", t=64)
                  P.dve(lambda e: e.tensor_scalar(out=sm[:, 0, :], in0=b3[:, :, 31], scalar1=-1.0, scalar2=None, op0=ALU.mult), r=["bcum"], w=["sm0"])
                  P.dve(lambda e: e.tensor_copy(out=sm[:, 1, :], in_=b3[:, :, 31]), r=["bcum"], w=["sm1"])
                  P.dve(lambda e: e.tensor_copy(out=sm[:, 2, :], in_=b3[:, :, 63]), r=["bcum"], w=["sm2"])
                  P.dve(lambda e: e.tensor_sub(out=sm[:, 3, :], in0=sm[:, 2, :], in1=sm[:, 1, :]), r=["sm1", "sm2"], w=["sm3"])
                  for c in range(8):
                      P.act(lambda e, c=c: e.activation(out=e1[:, c * 64:(c + 1) * 64], in_=bcum[:, c * 64:(c + 1) * 64], func=AF.Exp, bias=sm[:, 0, c:c + 1], scale=1.0), r=["bcum", "sm0"], w=["e1"])
                      P.act(lambda e, c=c: e.activation(out=e2[:, c * 64:(c + 1) * 64], in_=bcum[:, c * 64:(c + 1) * 64], func=AF.Exp, bias=sm[:, 1, c:c + 1], scale=-1.0), r=["bcum", "sm1"], w=["e2"])
                  P.act(lambda e: e.activation(out=eb[:], in_=bcum[:], func=AF.Exp), r=["bcum"], w=["eb"])
                  P.act(lambda e: e.activation(out=sm[:, 3, :], in_=sm[:, 3, :], func=AF.Exp), r=["sm3"], w=["sm3"])
                  P.act(lambda e: e.activation(out=sm[:, 4, :], in_=sm[:, 2, :], func=AF.Exp), r=["sm2"], w=["sm4"])
                  P.dve(lambda e: e.tensor_mul(out=qt[:], in0=qs[:], in1=e1[:]), r=["qs", "e1"], w=["qt"])
                  P.dve(lambda e: e.tensor_mul(out=kt[:], in0=kk[:], in1=e2[:]), r=["kk", "e2"], w=["kt"])
                  P.dve(lambda e: e.tensor_mul(out=qh[:], in0=qs[:], in1=eb[:]), r=["qs", "eb"], w=["qh"])
                  P.dma(lambda e, h=h, T0=T0: e.dma_start(out=qhat[h, :, T0:T0 + 512], in_=qh[:]), r=["qh"])
                  P.dve(lambda e, h=h, g=g: e.tensor_copy(out=aall[:, (g * 4 + h) * 8:(g * 4 + h + 1) * 8], in_=sm[:, 4, :]), r=["sm4"], w=["aall"])
                  for c in range(8):
                      P.pe(lambda e, c=c: e.matmul(pw[0:64, c * 64:(c + 1) * 64], kt[:, c * 64:(c + 1) * 64], qt[:, c * 64:(c + 1) * 64], start=True, stop=True), r=["kt", "qt"], w=["pw"])
                  P.dve(lambda e: e.tensor_tensor(out=AT[:], in0=pw[0:64, :], in1=cmask[:], op=ALU.mult), r=["pw", "cmask"], w=["AT"])
                  for c in range(8):
                      P.pe(lambda e, c=c: e.transpose(ptr[0:64, c * 128:(c + 1) * 128], kt[:, c * 64:(c + 1) * 64], idb[:]), r=["kt", "idb"], w=["ptr"])
                  P.act(lambda e: e.copy(out=ktok[:], in_=ptr[0:64, :].rearrange("p (c k) -> p c k", c=8)), r=["ptr"], w=["ktok"])
                  for c in range(8):
                      P.pe(lambda e, c=c, h=h: e.matmul(pbA[0:64, c * 128:(c + 1) * 128], AT[:, c * 64:(c + 1) * 64], vtok[:, c, h * 128:(h + 1) * 128], start=True, stop=True), r=["AT", "vtok"], w=["pbA"])
                  P.act(lambda e: e.copy(out=oist[:], in_=pbA[0:64, :].rearrange("p (c v) -> p c v", c=8)), r=["pbA"], w=["oist"])
                  P.dma(lambda e, h=h, g=g: e.dma_start(out=o_intra[g, h, :, :], in_=oist[:].rearrange("p c v -> p (c v)")), r=["oist"])
                  for c in range(8):
                      P.pe(lambda e, c=c, h=h: e.matmul(pbB[:, c * 128:(c + 1) * 128], ktok[:, c, :], vtok[:, c, h * 128:(h + 1) * 128], start=True, stop=True), r=["ktok", "vtok"], w=["pbB"])
                  P.dve(lambda e: e.tensor_tensor(out=kvst[:], in0=pbB[:].rearrange("p (c v) -> p c v", c=8), in1=sm[:, 3, :].unsqueeze(2).to_broadcast([128, 8, 128]), op=ALU.mult), r=["pbB", "sm3"], w=["kvst"])
                  P.dma(lambda e, h=h, g=g: e.dma_start(out=kv_out[g, h, :, :], in_=kvst[:].rearrange("p c v -> p (c v)")), r=["kvst"], q="pool")
              chk(5)
              for blk in range(4):
                  t0 = blk * 128
                  proj_t(t0, 128, C_HG, 512, pt[:, :], "pt")
                  P.act(lambda e: e.copy(out=gst[:], in_=pt[:]), r=["pt"], w=["gst"])
                  P.dma(lambda e, t0=t0, T0=T0: e.dma_start(out=g_out[T0 + t0:T0 + t0 + 128, :], in_=gst[:]), r=["gst"])
                  chk(5.1)
                  proj_t(t0, 128, C_SV, 256, pt[:, 0:256], "pt")
                  proj_t(t0, 128, C_IK, 72, pt[:, 256:328], "pt")
                  chk(5.2)
                  P.act(lambda e: e.copy(out=vst[:, :, 0:64], in_=pt[:, 0:256].rearrange("p (h d) -> p h d", h=4)), r=["pt"], w=["vst"])
                  P.act(lambda e: e.copy(out=iwst[:, 0:8], in_=pt[:, 320:328]), r=["pt"], w=["iwst"])
                  chk(5.3)
                  P.dma(lambda e, t0=t0, T0=T0: e.dma_start(out=V[T0 + t0:T0 + t0 + 128, :], in_=vst[:].rearrange("p h e -> p (h e)")), r=["vst"])
                  P.dma(lambda e, t0=t0, T0=T0: e.dma_start(out=iw_out[T0 + t0:T0 + t0 + 128, :], in_=iwst[:]), r=["iwst"])
              chk(6)
              for pair in range(2):
                  proj_f(C_SQ + pair * 128, 128)
                  head_norm(pf[:], "pf", 0, qn[:], "qn", hn_a[:], hn_b[:])
                  P.dma(lambda e, pair=pair, T0=T0: e.dma_start(out=QT[pair, :, T0:T0 + 512], in_=qn[:]), r=["qn"])
                  proj_f(C_SK + pair * 128, 128)
                  head_norm(pf[:], "pf", 1, qn[:], "qn", hn_a[:], hn_b[:])
                  P.dma(lambda e, pair=pair, T0=T0: e.dma_start(out=KT[pair, :, T0:T0 + 512], in_=qn[:]), r=["qn"])
                  proj_f(C_MQ + pair * 128, 128)
                  head_norm(pf[:], "pf", 2, mqT[:, pair, :], "mqT", hn_a[:], hn_b[:])
              for ih in range(8):
                  proj_f(C_IQ + ih * 64, 64)
                  P.act(lambda e, ih=ih: e.copy(out=iqst[:, ih, :], in_=pf[0:64, :]), r=["pf"], w=["iqst"])
              P.dma(lambda e, T0=T0: e.dma_start(out=iqT[:, :, T0:T0 + 512], in_=iqst[:]), r=["iqst"])
              proj_f(C_IK, 64)
              P.act(lambda e: e.copy(out=ikst[:], in_=pf[0:64, :]), r=["pf"], w=["ikst"])
              P.dma(lambda e, T0=T0: e.dma_start(out=ikT[:, T0:T0 + 512], in_=ikst[:]), r=["ikst"])
              chk(7)
              for mt in range(2):
                  for h in range(4):
                      po = (h % 2) * 64
                      P.pe(lambda e, mt=mt, h=h, po=po: e.matmul(pw[:], kmT[po:po + 64, h // 2, mt * 128:(mt + 1) * 128], mqT[po:po + 64, h // 2, :], start=True, stop=True), r=["kmT", "mqT"], w=["pw"])
                      P.act(lambda e, mt=mt, h=h: e.activation(out=PT[:, mt, h, :], in_=pw[:], func=AF.Exp), r=["pw"], w=["PT"])
              for blk in range(4):
                  t0 = blk * 128
                  for h in range(4):
                      for mt in range(2):
                          P.pe(lambda e, mt=mt, h=h, t0=t0: e.matmul(pt[:, h * 128:h * 128 + 65], PT[:, mt, h, t0:t0 + 128], vm1[:, mt, h, 0:65], start=(mt == 0), stop=(mt == 1)), r=["PT", "vm1"], w=["pt"])
                  pv = pt[:].rearrange("p (h e) -> p h e", e=128)
                  P.dve(lambda e, pv=pv: e.reciprocal(out=rc[:], in_=pv[:, :, 64]), r=["pt"], w=["rc"])
                  P.dve(lambda e, pv=pv: e.tensor_tensor(out=omst[:], in0=pv[:, :, 0:64], in1=rc[:].unsqueeze(2).to_broadcast([128, 4, 64]), op=ALU.mult), r=["pt", "rc"], w=["omst"])
                  P.dma(lambda e, t0=t0, T0=T0: e.dma_start(out=omem[T0 + t0:T0 + t0 + 128, :], in_=omst[:].rearrange("p h d -> p (h d)")), r=["omst"])
      except _Stop:
        pass
      P.dma(lambda e: e.dma_start(out=a_out[:, :], in_=aall[:]), r=["aall"])
      P.emit()
    return nc


def consts_A():
    ident = np.eye(128, dtype=np.float32)
    bd = np.zeros((128, 128), np.float32); bd[:64, :64] = 1.0 / 64; bd[64:, 64:] = 1.0 / 64
    cm = np.triu(np.ones((64, 64), np.float32))
    cmask = np.tile(cm, (1, 8))
    rmask = np.ones((128, 512), np.float32); rmask[:, ::64] = 0.0
    return dict(ident=ident, bd=bd, cmask=cmask, rmask=rmask)

NSLOT = 16
NITER = 20
RANGE = 256.0


def build_B(NS=NSLOT):
    nc = bass.Bass("TRN2", target_bir_lowering=False)
    def din(name, shape, dt=F32): return nc.dram_tensor(name, shape, dt, kind="ExternalInput").ap()
    def dout(name, shape, dt=F32): return nc.dram_tensor(name, shape, dt, kind="ExternalOutput").ap()
    QT = din("QT", [128, 2, 2048], BF16)
    iqs = din("iqs", [16, 64, 1024], BF16)
    iwp = din("iwp", [128, 128])
    KTg = din("KTg", [128, 2, 16384], BF16)
    Vg = din("Vg", [128, 128, 320], BF16)
    ikT = din("ikT", [64, 16384], BF16)
    CB = din("CB", [128, 1024])
    ident = din("ident", [128, 128])
    a_s = din("a_s", [128, 256]); kv_s = din("kv_s", [128, 64, 256])
    S_out = dout("S_out", [128, 64, 256])
    o_sa = dout("o_sa", [2048, 256])
    P = Prog(nc)
    with contextlib.ExitStack() as st:
        def sb(name, shape, dt=F32): return st.enter_context(nc.sbuf_tensor(name, shape, dt))
        def ps(name, shape, dt=F32): return st.enter_context(nc.psum_tensor(name, shape, dt))
        score = sb("score", [128, 16384])
        junk = sb("junk", [128, 4096], BF16)
        ikt = sb("ikt", [64, 16384], BF16)
        qt = sb("qt", [128, 2, 2048], BF16)
        iq = sb("iq", [64, 1024], BF16)
        iwt = sb("iwt", [128, 128]); sgn = sb("sgn", [128, 128]); absw = sb("absw", [128, 128])
        cb = sb("cb", [128, 1024]); idb = sb("idb", [128, 128], BF16)
        D = sb("D", [128, 8, 128], BF16)
        T = [sb(f"T{h}", [128, 512], BF16) for h in range(8)]
        ktcs = [sb(f"ktc{i}", [128, 2, 2048], BF16) for i in range(2)]; vcs = [sb(f"vc{i}", [128, 16, 320], BF16) for i in range(2)]
        mbcs = [sb(f"mbc{i}", [128, 512], BF16) for i in range(2)]; pTs = [sb(f"pT{i}", [128, 512], BF16) for i in range(2)]
        sm = sb("sm", [128, 16]); cnt4 = sb("cnt4", [128, 4])
        ost = sb("ost", [128, 4, 64]); rc = sb("rc", [128, 4])
        asb = sb("asb", [128, 256])
        zl = sb("zl", [128, 128], BF16); zr = sb("zr", [128, 512], BF16)
        P.dve(lambda e: e.memset(zl[:], 0.0), w=["zl"])
        P.dve(lambda e: e.memset(zr[:], 0.0), w=["zr"])
        pS = [ps(f"pS{i}", [128, 512]) for i in range(3)]
        pSc = ps("pSc", [128, 512])
        pL = [ps(f"pL{i}", [128, 512]) for i in range(2)]
        pO = ps("pO", [128, 512])
        P.dma(lambda e: e.dma_start(out=ikt[:], in_=ikT[:, :]), w=["ikt"])
        P.dma(lambda e: e.dma_start(out=qt[:], in_=QT[:, :, :]), w=["qt"])
        P.dma(lambda e: e.dma_start(out=iwt[:], in_=iwp[:, :]), w=["iwt"])
        P.dma(lambda e: e.dma_start(out=cb[:], in_=CB[:, :]), w=["cb"])
        P.dma(lambda e: e.dma_start(out=idb[:], in_=ident[:, :]), w=["idb"], q="pool")
        P.dma(lambda e: e.dma_start(out=asb[:], in_=a_s[:, :]), w=["asb"])
        for half in range(2):
            kin = score[:, 0:8192].rearrange("p (v c) -> p v c", c=256)
            kout = score[:, 8192:16384].rearrange("p (v c) -> p v c", c=256)
            P.dma(lambda e, half=half, kin=kin: e.dma_start(out=kin, in_=kv_s[:, half * 32:(half + 1) * 32, :]), w=["score"])
            for v in range(32):
                P.dve(lambda e, v=v: e.tensor_tensor_scan(out=score[:, 8192 + v * 256:8192 + (v + 1) * 256], data0=asb[:], data1=score[:, v * 256:(v + 1) * 256], initial=0.0, op0=ALU.mult, op1=ALU.add), r=["asb", "score"], w=["score"])
            P.dma(lambda e, half=half, kout=kout: e.dma_start(out=S_out[:, half * 32:(half + 1) * 32, :], in_=kout), r=["score"])
        P.act(lambda e: e.activation(out=sgn[:], in_=iwt[:], func=AF.Sign), r=["iwt"], w=["sgn"])
        P.dve(lambda e: e.tensor_mul(out=absw[:], in0=iwt[:], in1=sgn[:]), r=["iwt", "sgn"], w=["absw"])
        for i in range(NS):
            nk = 8 * (i + 1); nch = 2 * (i + 1); n = nk * 128
            q0 = i * 128
            P.dma(lambda e, i=i: e.dma_start(out=iq[:], in_=iqs[i, :, :]), w=["iq"])
            for h in range(8):
                P.dve(lambda e, h=h, i=i: e.tensor_scalar(out=D[:, h, :], in0=idb[:], scalar1=sgn[:, i * 8 + h:i * 8 + h + 1], scalar2=None, op0=ALU.mult), r=["idb", "sgn"], w=["D"])
            for kc in range(nch):
                for h in range(8):
                    b = (kc * 8 + h) % 3
                    P.pe(lambda e, h=h, kc=kc, b=b: e.matmul(pS[b][:], iq[:, h * 128:(h + 1) * 128], ikt[:, kc * 512:(kc + 1) * 512], start=True, stop=True), r=["iq", "ikt"], w=[f"pS{b}"])
                    col = i * 8 + h
                    if h % 2 == 0:
                        P.act(lambda e, h=h, b=b, col=col: e.activation(out=T[h][:], in_=pS[b][:], func=AF.Relu, scale=absw[:, col:col + 1]), r=[f"pS{b}", "absw"], w=[f"T{h}"])
                    else:
                        P.dve(lambda e, h=h, b=b, col=col: e.tensor_scalar(out=T[h][:], in0=pS[b][:], scalar1=absw[:, col:col + 1], scalar2=0.0, op0=ALU.mult, op1=ALU.max), r=[f"pS{b}", "absw"], w=[f"T{h}"])
                    P.pe(lambda e, h=h: e.matmul(pSc[:], D[:, h, :], T[h][:], start=(h == 0), stop=(h == 7)), r=["D", f"T{h}"], w=["pSc"])
                if kc >= nch - 2:
                    j = kc - (nch - 2)
                    P.dve(lambda e, kc=kc, j=j: e.tensor_tensor(out=score[:, kc * 512:(kc + 1) * 512], in0=pSc[:], in1=cb[:, j * 512:(j + 1) * 512], op=ALU.add), r=["pSc", "cb"], w=["score"])
                else:
                    P.dve(lambda e, kc=kc: e.tensor_copy(out=score[:, kc * 512:(kc + 1) * 512], in_=pSc[:]), r=["pSc"], w=["score"])
            P.dve(lambda e, n=n: e.reduce_max(out=sm[:, 1:2], in_=score[:, 0:n], axis=AX.X), r=["score"], w=["sm"])
            P.dve(lambda e: e.tensor_scalar(out=sm[:, 0:1], in0=sm[:, 1:2], scalar1=-RANGE, scalar2=None, op0=ALU.add), r=["sm"], w=["sm"])
            npc = (n + 4095) // 4096
            for it in range(NITER):
                wdt = RANGE / (2.0 ** (it + 1))
                P.dve(lambda e, wdt=wdt: e.tensor_scalar(out=sm[:, 2:3], in0=sm[:, 0:1], scalar1=wdt, scalar2=None, op0=ALU.add), r=["sm"], w=["sm"])
                P.dve(lambda e: e.memset(cnt4[:], 0.0), w=["cnt4"])
                for pc in range(npc):
                    c0 = pc * 4096; c1 = min(n, c0 + 4096)
                    P.dve(lambda e, c0=c0, c1=c1, pc=pc: e.tensor_scalar(out=junk[:, 0:c1 - c0], in0=score[:, c0:c1], scalar1=sm[:, 2:3], scalar2=0.0, op0=ALU.is_gt, op1=ALU.add, accum_out=cnt4[:, pc:pc + 1]), r=["score", "sm"], w=["junk", "cnt4"])
                P.dve(lambda e, npc=npc: e.reduce_sum(out=sm[:, 3:4], in_=cnt4[:, 0:npc], axis=AX.X), r=["cnt4"], w=["sm"])
                P.dve(lambda e, wdt=wdt: e.tensor_scalar(out=sm[:, 4:5], in0=sm[:, 3:4], scalar1=255.5, scalar2=wdt, op0=ALU.is_gt, op1=ALU.mult), r=["sm"], w=["sm"])
                P.dve(lambda e: e.tensor_add(out=sm[:, 0:1], in0=sm[:, 0:1], in1=sm[:, 4:5]), r=["sm"], w=["sm"])
            P.pe(lambda e: e.matmul(pO[:], zl[:], zr[:], start=True, stop=False), r=["zl", "zr"], w=["pO"])
            for kc in range(nch):
                wi = (kc // 4) % 2
                ktc = ktcs[wi]; vc = vcs[wi]; mbc = mbcs[kc % 2]
                if kc % 4 == 0:
                    P.dma(lambda e, kc=kc, ktc=ktc: e.dma_start(out=ktc[:], in_=KTg[:, :, kc * 512:kc * 512 + 2048]), w=[f"ktc{wi}"])
                    P.dma(lambda e, kc=kc, vc=vc: e.dma_start(out=vc[:], in_=Vg[:, kc * 4:kc * 4 + 16, :]), w=[f"vc{wi}"], q="pool")
                P.dve(lambda e, kc=kc, mbc=mbc: e.tensor_scalar(out=mbc[:], in0=score[:, kc * 512:(kc + 1) * 512], scalar1=sm[:, 0:1], scalar2=-30000.0, op0=ALU.is_le, op1=ALU.mult), r=["score", "sm"], w=[f"mbc{kc % 2}"])
                for t in range(4):
                    kt = kc * 4 + t
                    lt = (kc % 4) * 4 + t
                    b = kt % 2
                    pT = pTs[b]
                    for h in range(4):
                        po = (h % 2) * 64
                        P.pe(lambda e, h=h, po=po, lt=lt, b=b, q0=q0, ktc=ktc: e.matmul(pL[b][:, h * 128:(h + 1) * 128], ktc[po:po + 64, h // 2, lt * 128:(lt + 1) * 128], qt[po:po + 64, h // 2, q0:q0 + 128], start=True, stop=False), r=[f"ktc{wi}", "qt"], w=[f"pL{b}"])
                        P.pe(lambda e, h=h, t=t, b=b, mbc=mbc: e.matmul(pL[b][:, h * 128:(h + 1) * 128], mbc[:, t * 128:(t + 1) * 128], idb[:], start=False, stop=True), r=[f"mbc{kc % 2}", "idb"], w=[f"pL{b}"])
                    P.act(lambda e, b=b, pT=pT: e.activation(out=pT[:], in_=pL[b][:], func=AF.Exp), r=[f"pL{b}"], w=[f"pT{b}"])
                    for h in range(4):
                        P.pe(lambda e, h=h, lt=lt, kt=kt, nk=nk, pT=pT, vc=vc: e.matmul(pO[:, h * 128:h * 128 + 65], pT[:, h * 128:(h + 1) * 128], vc[:, lt, h * 80:h * 80 + 65], start=False, stop=(kt == nk - 1)), r=[f"pT{b}", f"vc{wi}"], w=["pO"])
            pv = pO[:].rearrange("p (h e) -> p h e", e=128)
            P.dve(lambda e, pv=pv: e.reciprocal(out=rc[:], in_=pv[:, :, 64]), r=["pO"], w=["rc"])
            P.dve(lambda e, pv=pv: e.tensor_tensor(out=ost[:], in0=pv[:, :, 0:64], in1=rc[:].unsqueeze(2).to_broadcast([128, 4, 64]), op=ALU.mult), r=["pO", "rc"], w=["ost"])
            P.dma(lambda e, q0=q0: e.dma_start(out=o_sa[q0:q0 + 128, :], in_=ost[:].rearrange("p h d -> p (h d)")), r=["ost"])
        P.emit()
    return nc


EPS = 1e-6
NT = 2048
NG = 4


def build_C():
    nc = bass.Bass("TRN2", target_bir_lowering=False)
    def din(name, shape, dt=F32): return nc.dram_tensor(name, shape, dt, kind="ExternalInput").ap()
    def dout(name, shape, dt=F32): return nc.dram_tensor(name, shape, dt, kind="ExternalOutput").ap()
    x = din("x", [NT, 1024]); o_intra = din("o_intra", [NG, 4, 64, 1024]); qhat = din("qhat", [4, 128, NT], BF16)
    Sp = din("Sp", [NG, 128, 4096]); g_in = din("g_in", [NT, 512]); omem = din("omem", [NT, 256]); o_sa = din("o_sa", [NT, 256])
    w_out = din("w_out", [128, 8192]); w_fi = din("w_fi", [44, 128, 1024]); w_fo = din("w_fo", [128, 22 * 1024])
    gffn = din("gffn", [128, 1024]); ghg = din("ghg", [64, 512]); ident = din("ident", [128, 128])
    x_out = dout("x_out", [NT, 1024])
    P = Prog(nc)
    with contextlib.ExitStack() as st:
        def sb(name, shape, dt=F32): return st.enter_context(nc.sbuf_tensor(name, shape, dt))
        def ps(name, shape, dt=F32): return st.enter_context(nc.psum_tensor(name, shape, dt))
        wout = sb("wout", [128, 8, 1024], BF16); wfo = sb("wfo", [128, 22, 1024], BF16)
        wt = [sb(f"wt{i}", [128, 8, 128], BF16) for i in range(4)]
        gf = sb("gf", [128, 1024]); gh = sb("gh", [64, 4, 128]); idb = sb("idb", [128, 128], BF16)
        uT = sb("uT", [128, 22, 512], BF16)
        xg = sb("xg", [128, 4, 1024])
        h2T = sb("h2T", [128, 8, 512], BF16); mixT = sb("mixT", [128, 8, 512], BF16)
        qh = sb("qh", [128, 4, 512], BF16); spg = sb("spg", [128, 4, 8, 128], BF16)
        oig = sb("oig", [64, 4, 8, 128])
        o32 = sb("o32", [64, 4, 128]); gch = sb("gch", [64, 512]); mixb = sb("mixb", [64, 1024], BF16)
        junk = sb("junk", [128, 1024], BF16); epsb = sb("epsb", [128, 1]); P.dve(lambda e: e.memset(epsb[:], EPS), w=["epsb"]); ss = sb("ss", [128, 8])
        hb = sb("hb", [128, 1024], BF16); sa = sb("sa", [128, 512])
        ptr = ps("ptr", [128, 1024], BF16)
        pf = ps("pf", [128, 512]); pw = ps("pw", [128, 512]); pt = ps("pt", [128, 512]); po = ps("po", [128, 512])
        for q4 in range(4):
            P.dma(lambda e, q4=q4: e.dma_start(out=wout[:, q4 * 2:(q4 + 1) * 2, :], in_=w_out[:, q4 * 2048:(q4 + 1) * 2048]), w=["wout"], q="pool")
        for q11 in range(11):
            P.dma(lambda e, q11=q11: e.dma_start(out=wfo[:, q11 * 2:(q11 + 1) * 2, :], in_=w_fo[:, q11 * 2048:(q11 + 1) * 2048]), w=["wfo"], q="pool")
        P.dma(lambda e: e.dma_start(out=gf[:], in_=gffn[:, :]), w=["gf"])
        P.dma(lambda e: e.dma_start(out=gh[:], in_=ghg[:, :]), w=["gh"])
        P.dma(lambda e: e.dma_start(out=idb[:], in_=ident[:, :]), w=["idb"], q="pool")
        wti = 0
        for g in range(NG):
            T0 = g * 512
            for h in range(4):
                P.dma(lambda e, h=h, T0=T0: e.dma_start(out=qh[:, h, :], in_=qhat[h, :, T0:T0 + 512]), w=["qh"])
            P.dma(lambda e, g=g: e.dma_start(out=spg[:], in_=Sp[g, :, :]), w=["spg"], q="pool")
            P.dma(lambda e, g=g: e.dma_start(out=oig[:], in_=o_intra[g, :, :, :].rearrange("h p f -> p h f")), w=["oig"])
            for c in range(8):
                r0 = T0 + c * 64
                for h in range(4):
                    P.pe(lambda e, h=h, c=c: e.matmul(pw[0:64, h * 128:(h + 1) * 128], qh[:, h, c * 64:(c + 1) * 64], spg[:, h, c, :], start=True, stop=True), r=["qh", "spg"], w=["pw"])
                P.dve(lambda e, c=c: e.tensor_tensor(out=o32[:], in0=pw[0:64, :].rearrange("p (h v) -> p h v", h=4), in1=oig[:, :, c, :], op=ALU.add), r=["pw", "oig"], w=["o32"])
                for h in range(4):
                    P.act(lambda e, h=h: e.activation(out=junk[0:64, 0:128], in_=o32[:, h, :], func=AF.Square, accum_out=ss[0:64, h:h + 1]), r=["o32"], w=["junk", "ss"])
                P.act(lambda e: e.activation(out=ss[0:64, 4:8], in_=ss[0:64, 0:4], func=AF.Sqrt, bias=epsb[0:64, 0:1], scale=1.0 / 128), r=["ss", "epsb"], w=["ss"])
                P.dve(lambda e: e.reciprocal(out=ss[0:64, 4:8], in_=ss[0:64, 4:8]), r=["ss"], w=["ss"])
                P.dma(lambda e, r0=r0: e.dma_start(out=gch[:], in_=g_in[r0:r0 + 64, :]), w=["gch"])
                P.act(lambda e: e.activation(out=gch[:], in_=gch[:], func=AF.Silu), r=["gch"], w=["gch"])
                P.dve(lambda e: e.tensor_tensor(out=o32[:], in0=o32[:], in1=ss[0:64, 4:8].unsqueeze(2).to_broadcast([64, 4, 128]), op=ALU.mult), r=["o32", "ss"], w=["o32"])
                P.dve(lambda e: e.tensor_tensor(out=o32[:], in0=o32[:], in1=gh[:], op=ALU.mult), r=["o32", "gh"], w=["o32"])
                P.dma(lambda e, r0=r0: e.dma_start(out=mixb[:, 512:768], in_=o_sa[r0:r0 + 64, :]), w=["mixb"], q="pool")
                P.dma(lambda e, r0=r0: e.dma_start(out=mixb[:, 768:1024], in_=omem[r0:r0 + 64, :]), w=["mixb"], q="pool")
                P.dve(lambda e: e.tensor_tensor(out=mixb[:, 0:512], in0=o32[:].rearrange("p h v -> p (h v)"), in1=gch[:], op=ALU.mult), r=["o32", "gch"], w=["mixb"])
                for kc in range(8):
                    P.pe(lambda e, kc=kc: e.transpose(ptr[:, kc * 64:(kc + 1) * 64], mixb[:, kc * 128:(kc + 1) * 128], idb[0:64, 0:64]), r=["mixb", "idb"], w=["ptr"])
                P.act(lambda e, c=c: e.copy(out=mixT[:, :, c * 64:(c + 1) * 64], in_=ptr[:, 0:512].rearrange("p (k t) -> p k t", k=8)), r=["ptr"], w=["mixT"])
            for blk in range(4):
                r0 = T0 + blk * 128
                P.dma(lambda e, r0=r0, blk=blk: e.dma_start(out=xg[:, blk, :], in_=x[r0:r0 + 128, :]), w=[f"xg{blk}"])
                for half in range(2):
                    for kc in range(8):
                        P.pe(lambda e, kc=kc, half=half, blk=blk: e.matmul(pt[:], mixT[:, kc, blk * 128:(blk + 1) * 128], wout[:, kc, half * 512:(half + 1) * 512], start=(kc == 0), stop=(kc == 7)), r=["mixT", "wout"], w=["pt"])
                    P.dve(lambda e, half=half, blk=blk: e.tensor_tensor(out=xg[:, blk, half * 512:(half + 1) * 512], in0=xg[:, blk, half * 512:(half + 1) * 512], in1=pt[:], op=ALU.add), r=["pt", f"xg{blk}"], w=[f"xg{blk}"])
                P.act(lambda e, blk=blk: e.activation(out=junk[:], in_=xg[:, blk, :], func=AF.Square, accum_out=ss[:, 0:1]), r=[f"xg{blk}"], w=["junk", "ss"])
                P.act(lambda e: e.activation(out=ss[:, 1:2], in_=ss[:, 0:1], func=AF.Sqrt, bias=epsb[:, 0:1], scale=1.0 / 1024), r=["ss", "epsb"], w=["ss"])
                P.dve(lambda e: e.reciprocal(out=ss[:, 1:2], in_=ss[:, 1:2]), r=["ss"], w=["ss"])
                P.dve(lambda e, blk=blk: e.scalar_tensor_tensor(out=hb[:], in0=xg[:, blk, :], scalar=ss[:, 1:2], in1=gf[:], op0=ALU.mult, op1=ALU.mult), r=[f"xg{blk}", "ss", "gf"], w=["hb"])
                for kc in range(8):
                    P.pe(lambda e, kc=kc: e.transpose(ptr[:, kc * 128:(kc + 1) * 128], hb[:, kc * 128:(kc + 1) * 128], idb[:]), r=["hb", "idb"], w=["ptr"])
                P.act(lambda e, blk=blk: e.copy(out=h2T[:, :, blk * 128:(blk + 1) * 128], in_=ptr[:].rearrange("p (k t) -> p k t", k=8)), r=["ptr"], w=["h2T"])
            for j in range(22):
                ia = wti % 4; ib = (wti + 1) % 4; wti += 2
                P.dma(lambda e, j=j, ia=ia: e.dma_start(out=wt[ia][:], in_=w_fi[j, :, :]), w=[f"wt{ia}"], q="pool")
                P.dma(lambda e, j=j, ib=ib: e.dma_start(out=wt[ib][:], in_=w_fi[22 + j, :, :]), w=[f"wt{ib}"], q="pool")
                for kc in range(8):
                    P.pe(lambda e, kc=kc, ia=ia: e.matmul(pf[:], wt[ia][:, kc, :], h2T[:, kc, :], start=(kc == 0), stop=(kc == 7)), r=[f"wt{ia}", "h2T"], w=["pf"])
                for kc in range(8):
                    P.pe(lambda e, kc=kc, ib=ib: e.matmul(po[:], wt[ib][:, kc, :], h2T[:, kc, :], start=(kc == 0), stop=(kc == 7)), r=[f"wt{ib}", "h2T"], w=["po"])
                P.act(lambda e: e.activation(out=sa[:], in_=pf[:], func=AF.Silu), r=["pf"], w=["sa"])
                P.dve(lambda e, j=j: e.tensor_tensor(out=uT[:, j, :], in0=sa[:], in1=po[:], op=ALU.mult), r=["sa", "po"], w=["uT"])
            for blk in range(4):
                r0 = T0 + blk * 128
                for half in range(2):
                    for j in range(22):
                        P.pe(lambda e, j=j, half=half, blk=blk: e.matmul(pt[:], uT[:, j, blk * 128:(blk + 1) * 128], wfo[:, j, half * 512:(half + 1) * 512], start=(j == 0), stop=(j == 21)), r=["uT", "wfo"], w=["pt"])
                    P.dve(lambda e, half=half, blk=blk: e.tensor_tensor(out=xg[:, blk, half * 512:(half + 1) * 512], in0=xg[:, blk, half * 512:(half + 1) * 512], in1=pt[:], op=ALU.add), r=["pt", f"xg{blk}"], w=[f"xg{blk}"])
                P.dma(lambda e, r0=r0, blk=blk: e.dma_start(out=x_out[r0:r0 + 128, :], in_=xg[:, blk, :]), r=[f"xg{blk}"])
        P.emit()
    return nc


CORES = list(range(8))


def _shard_tok(a, c):
    return np.ascontiguousarray(a.reshape(16, 8, 128, -1)[:, c].reshape(2048, -1))


def _c(a):
    return np.ascontiguousarray(a)


def make_CBs():
    CBs = []
    for c in range(8):
        cb = np.zeros((128, 8, 128), np.float32)
        cb[:, c + 1:, :] = -1e30
        tri = np.where(np.arange(128)[None, :] <= np.arange(128)[:, None], 0.0, -1e30).astype(np.float32)
        cb[:, c, :] = tri
        CBs.append(_c(cb.reshape(128, 1024)))
    return CBs


def make_A_inputs(inp, xs, layer, cA):
    base = dict(cA)
    base["w_in"] = _c(inp["w_in"][layer]); base["gmix"] = _c(np.broadcast_to(inp["norm_mix"][layer], (128, 1024)))
    base["mem"] = _c(inp["mem"][0]); base["gmem"] = _c(np.broadcast_to(inp["norm_mem"][layer], (128, 1024)))
    base["w_kv"] = _c(inp["w_mem_kv"][layer])
    base["lbl"] = _c(inp["lb_logits"].reshape(2, 4, 128).transpose(2, 0, 1).reshape(128, 8))
    base["lsel"] = np.full((128, 1), float(layer), np.float32)
    base["gcols"] = _c(np.stack([np.tile(inp["sa_q_gain"][layer], 2), np.tile(inp["sa_k_gain"][layer], 2),
                                 np.tile(inp["mem_q_gain"][layer], 2), np.tile(inp["mem_k_gain"][layer], 2)], axis=1))
    return [dict(base, x=xs[c]) for c in CORES]


def make_B_inputs(ra, CBs, ident):
    KTg = np.stack([np.asarray(ra[c]["KT"]).reshape(2, 128, 16, 128) for c in CORES], axis=3).reshape(2, 128, 16384)
    KTg = _c(KTg.transpose(1, 0, 2))
    Vg = np.stack([np.asarray(ra[c]["V"]).reshape(16, 128, 320) for c in CORES], axis=1).reshape(128, 128, 320)
    Vg = _c(Vg.transpose(1, 0, 2))
    ikg = _c(np.stack([np.asarray(ra[c]["ikT"]).reshape(64, 16, 128) for c in CORES], axis=2).reshape(64, 16384))
    a_glob = np.stack([np.asarray(ra[c]["a_out"]).reshape(128, 4, 4, 4, 2).transpose(0, 2, 1, 3, 4).reshape(128, 4, 16, 2) for c in CORES], axis=3)
    a_glob = a_glob.reshape(128, 4, 256).transpose(1, 0, 2)
    kv_glob = np.stack([np.asarray(ra[c]["kv_out"]).reshape(4, 4, 128, 4, 2, 128).transpose(1, 2, 0, 3, 4, 5).reshape(4, 128, 16, 2, 128) for c in CORES], axis=3)
    kv_glob = kv_glob.reshape(4, 128, 256, 128)
    maps = []
    for c in CORES:
        h, half = c // 2, c % 2
        d = dict(ident=ident, KTg=KTg, Vg=Vg, ikT=ikg, CB=CBs[c])
        d["QT"] = _c(np.asarray(ra[c]["QT"]).transpose(1, 0, 2))
        d["iqs"] = _c(np.asarray(ra[c]["iqT"]).reshape(64, 8, 16, 128).transpose(2, 0, 1, 3).reshape(16, 64, 1024))
        d["iwp"] = _c(np.asarray(ra[c]["iw_out"])[:, :8].reshape(16, 128, 8).transpose(1, 0, 2).reshape(128, 128))
        d["a_s"] = _c(a_glob[h])
        d["kv_s"] = _c(kv_glob[h][:, :, half * 64:(half + 1) * 64].transpose(0, 2, 1))
        maps.append(d)
    return maps


def make_C_inputs(inp, layer, xs, ra, rb, ident):
    Sg = np.stack([np.concatenate([np.asarray(rb[2 * h]["S_out"]), np.asarray(rb[2 * h + 1]["S_out"])], axis=1) for h in range(4)], axis=0)
    Sprev = np.concatenate([np.zeros_like(Sg[..., :1]), Sg[..., :-1]], axis=-1)
    Sr = Sprev.reshape(4, 128, 128, 16, 8, 2)
    wo = _c(inp["w_out"][layer].reshape(8, 128, 1024).transpose(1, 0, 2).reshape(128, 8192))
    wfi = _c(inp["w_ffn_in"][layer].reshape(8, 128, 44, 128).transpose(2, 1, 0, 3).reshape(44, 128, 1024))
    wfo = _c(inp["w_ffn_out"][layer].reshape(22, 128, 1024).transpose(1, 0, 2).reshape(128, 22 * 1024))
    gffn = _c(np.broadcast_to(inp["norm_ffn"][layer], (128, 1024)))
    ghg = _c(np.broadcast_to(np.tile(inp["hg_out_gain"][layer], 4), (64, 512)))
    maps = []
    for c in CORES:
        sp = Sr[:, :, :, :, c, :].reshape(4, 128, 128, 4, 4, 2).transpose(3, 1, 0, 4, 5, 2).reshape(4, 128, 4096)
        d = dict(x=xs[c], o_intra=np.asarray(ra[c]["o_intra"]), qhat=np.asarray(ra[c]["qhat"]), Sp=_c(sp),
                 g_in=np.asarray(ra[c]["g_out"]), omem=np.asarray(ra[c]["omem"]), o_sa=np.asarray(rb[c]["o_sa"]),
                 w_out=wo, w_fi=wfi, w_fo=wfo, gffn=gffn, ghg=ghg, ident=ident)
        maps.append(d)
    return maps


def kernel(**inp):
    inp = {k: np.asarray(v) for k, v in inp.items()}
    x = inp["x"][0]
    xs = [_shard_tok(x, c) for c in CORES]
    cA = consts_A()
    ident = cA["ident"]
    ncA = build_A(); ncB = build_B(); ncC = build_C()
    CBs = make_CBs()
    for layer in range(2):
        ra = run_bass_kernel_spmd(ncA, make_A_inputs(inp, xs, layer, cA), core_ids=CORES).results
        rb = run_bass_kernel_spmd(ncB, make_B_inputs(ra, CBs, ident), core_ids=CORES).results
        rc_ = run_bass_kernel_spmd(ncC, make_C_inputs(inp, layer, xs, ra, rb, ident), core_ids=CORES).results
        xs = [np.asarray(rc_[c]["x_out"]) for c in CORES]
    out = np.stack([xs[c].reshape(16, 128, 1024) for c in CORES], axis=1).reshape(1, 16384, 1024)
    return out.astype(np.float32)
```

```python
import contextlib
import numpy as np
import concourse.bass as bass
import concourse.mybir as mybir
from concourse.bass_utils import run_bass_kernel_spmd

F32 = mybir.dt.float32
BF16 = mybir.dt.bfloat16
AF = mybir.ActivationFunctionType
ALU = mybir.AluOpType
AX = mybir.AxisListType

ENGS = ("pe", "act", "dve", "pool", "sp")


class Prog:
    def __init__(self, nc):
        self.nc = nc
        self.ops = []
        self.same_engine_sync = True
        self.max_outstanding = 3

    def op(self, eng, fn, reads=(), writes=(), dma=False):
        self.ops.append((eng, fn, tuple(reads), tuple(writes), dma))

    def pe(self, fn, r=(), w=()): self.op("pe", fn, r, w)
    def act(self, fn, r=(), w=()): self.op("act", fn, r, w)
    def dve(self, fn, r=(), w=()): self.op("dve", fn, r, w)
    def pool(self, fn, r=(), w=()): self.op("pool", fn, r, w)
    def dma(self, fn, r=(), w=(), q="sp"): self.op(q, fn, r, w, True)

    def emit(self):
        nc = self.nc
        ops = self.ops
        n = len(ops)
        R = self.max_outstanding
        stream = []
        prev_same = [None] * n if False else None
        qcount = {}
        for (e, _, _, _, d) in ops:
            if d:
                k = qcount.get(e, 0); qcount[e] = k + 1
                stream.append("%s_dma%d" % (e, k % R))
            else:
                stream.append(e)
        last_w = {}
        readers = {}
        deps = [set() for _ in range(n)]
        for i, (e, fn, rs, ws, d) in enumerate(ops):
            for k in rs:
                if k in last_w:
                    deps[i].add(last_w[k])
            for k in ws:
                if k in last_w:
                    deps[i].add(last_w[k])
                for j in readers.get(k, ()):
                    if j != i:
                        deps[i].add(j)
            for k in rs:
                readers.setdefault(k, []).append(i)
            for k in ws:
                last_w[k] = i
                readers[k] = []
        needed = [False] * n
        for i in range(n):
            keep = set()
            for j in deps[i]:
                if stream[j] == stream[i] == "pe":
                    continue
                if (not self.same_engine_sync) and stream[j] == stream[i] and "_dma" not in stream[i]:
                    continue
                keep.add(j)
            deps[i] = keep
        for i in range(n):
            for j in deps[i]:
                needed[j] = True
        for i in range(n):
            if ops[i][4]:
                needed[i] = True
        cnt = {}
        val = [0] * n
        for i in range(n):
            s = stream[i]
            if needed[i]:
                cnt[s] = cnt.get(s, 0) + (16 if "_dma" in s else 1)
            val[i] = cnt.get(s, 0)
        streams = sorted(set(stream))
        self.final_counts = dict(cnt)
        import contextlib
        with contextlib.ExitStack() as st:
            sems = {s: st.enter_context(nc.semaphore("sem_" + s)) for s in streams}
            block = st.enter_context(nc.Block())
            per_eng = {e: [i for i in range(n) if ops[i][0] == e] for e in ENGS}

            def run(engname, eng):
                waited = {}
                ndma = 0
                for i in per_eng[engname]:
                    if ops[i][4]:
                        s_ = stream[i]
                        lim = val[i] - 16
                        if lim > 0 and waited.get(s_, 0) < lim:
                            eng.wait_ge(sems[s_], lim)
                            waited[s_] = lim
                        ndma += 1
                    need = {}
                    for j in deps[i]:
                        s = stream[j]
                        need[s] = max(need.get(s, 0), val[j])
                    for s, v in need.items():
                        if waited.get(s, 0) < v:
                            eng.wait_ge(sems[s], v)
                            waited[s] = v
                    ins = ops[i][1](eng)
                    if needed[i]:
                        ins.then_inc(sems[stream[i]], 16 if ops[i][4] else 1)
                if engname == "sp":
                    for s in streams:
                        if "_dma" in s and cnt.get(s, 0) > 0:
                            eng.wait_ge(sems[s], cnt[s])

            @block.sync
            def _(e): run("sp", e)

            @block.gpsimd
            def _(e): run("pool", e)

            @block.scalar
            def _(e): run("act", e)

            @block.vector
            def _(e): run("dve", e)

            @block.tensor
            def _(e): run("pe", e)

EPS = 1e-6
NT = 2048
NG = 4
IN_W = 3656
C_HQ, C_HF, C_HI, C_HG, C_SQ, C_SK, C_SV, C_IQ, C_IK, C_IW, C_MQ = 0, 512, 1024, 1536, 2048, 2304, 2560, 2816, 3328, 3392, 3400


class _Stop(Exception): pass

def build_A(STOP=99.0):
    nc = bass.Bass("TRN2", target_bir_lowering=False)
    def din(name, shape, dt=F32): return nc.dram_tensor(name, shape, dt, kind="ExternalInput").ap()
    def dout(name, shape, dt=F32): return nc.dram_tensor(name, shape, dt, kind="ExternalOutput").ap()
    x = din("x", [NT, 1024]); w_in = din("w_in", [1024, IN_W]); gmix = din("gmix", [128, 1024])
    mem = din("mem", [256, 1024]); gmem = din("gmem", [128, 1024]); w_kv = din("w_kv", [1024, 512])
    lbl = din("lbl", [128, 8]); lsel = din("lsel", [128, 1]); gcols = din("gcols", [128, 4])
    ident = din("ident", [128, 128]); bdin = din("bd", [128, 128]); cmaskin = din("cmask", [64, 512]); rmaskin = din("rmask", [128, 512])
    o_intra = dout("o_intra", [NG, 4, 64, 1024]); qhat = dout("qhat", [4, 128, NT], BF16); g_out = dout("g_out", [NT, 512])
    kv_out = dout("kv_out", [NG, 4, 128, 1024]); a_out = dout("a_out", [128, NG * 4 * 8]); omem = dout("omem", [NT, 256])
    QT = dout("QT", [2, 128, NT], BF16); KT = dout("KT", [2, 128, NT], BF16); V = dout("V", [NT, 320], BF16)
    ikT = dout("ikT", [64, NT], BF16); iqT = dout("iqT", [64, 8, NT], BF16); iw_out = dout("iw_out", [NT, 128])
    P = Prog(nc)
    def chk(k):
        if k > STOP: raise _Stop()
    with contextlib.ExitStack() as st:
      try:
          def sb(name, shape, dt=F32): return st.enter_context(nc.sbuf_tensor(name, shape, dt))
          def ps(name, shape, dt=F32): return st.enter_context(nc.psum_tensor(name, shape, dt))
          wbf = sb("wbf", [128, 8, IN_W], BF16)
          gainb = sb("gainb", [128, 1024]); gmemb = sb("gmemb", [128, 1024])
          idb = sb("idb", [128, 128], BF16); bd = sb("bdb", [128, 128], BF16); cmask = sb("cmask_s", [64, 512], BF16); rmask = sb("rmask_s", [128, 512])
          lbt = sb("lbt", [128, 8]); lselt = sb("lselt", [128, 1]); lb = sb("lb", [128, 4]); oml = sb("oml", [128, 4]); gc = sb("gc", [128, 4])
          xin = [sb(f"xin{i}", [128, 1024]) for i in range(2)]
          hb = [sb(f"hb{i}", [128, 1024], BF16) for i in range(2)]
          junk = sb("junk", [128, 1024], BF16); epsb = sb("epsb", [128, 1]); P.dve(lambda e: e.memset(epsb[:], EPS), w=["epsb"])
          ss = sb("ss", [128, 4])
          hT = sb("hT", [128, 8, 512], BF16)
          ptr = ps("ptr", [128, 1024], BF16)
          pf = ps("pf", [128, 512]); pt = ps("pt", [128, 512]); pw = ps("pw", [128, 512])
          pbA = ps("pbA", [128, 1024]); pbB = ps("pbB", [128, 1024])
          for kc in range(8):
              P.dma(lambda e, kc=kc: e.dma_start(out=wbf[:, kc, :], in_=w_in[kc * 128:(kc + 1) * 128, :]), w=[f"wbf{kc}"], q="pool")
          P.dma(lambda e: e.dma_start(out=gainb[:], in_=gmix[:, :]), w=["gainb"])
          P.dma(lambda e: e.dma_start(out=gmemb[:], in_=gmem[:, :]), w=["gmemb"])
          P.dma(lambda e: e.dma_start(out=idb[:], in_=ident[:, :]), w=["idb"], q="pool")
          P.dma(lambda e: e.dma_start(out=bd[:], in_=bdin[:, :]), w=["bd"], q="pool")
          P.dma(lambda e: e.dma_start(out=cmask[:], in_=cmaskin[:, :]), w=["cmask"], q="pool")
          P.dma(lambda e: e.dma_start(out=rmask[:], in_=rmaskin[:, :]), w=["rmask"])
          P.dma(lambda e: e.dma_start(out=lbt[:], in_=lbl[:, :]), w=["lbt"])
          P.dma(lambda e: e.dma_start(out=lselt[:], in_=lsel[:, :]), w=["lselt"])
          P.dma(lambda e: e.dma_start(out=gc[:], in_=gcols[:, :]), w=["gc"])
          P.dve(lambda e: e.tensor_sub(out=lb[:], in0=lbt[:, 4:8], in1=lbt[:, 0:4]), r=["lbt"], w=["lb"])
          P.act(lambda e: e.activation(out=lb[:], in_=lb[:], func=AF.Sigmoid), r=["lb"], w=["lb"])
          P.dve(lambda e: e.tensor_scalar(out=lb[:], in0=lb[:], scalar1=lselt[:, 0:1], scalar2=None, op0=ALU.mult), r=["lb", "lselt"], w=["lb"])
          P.dve(lambda e: e.tensor_scalar(out=oml[:], in0=lb[:], scalar1=-1.0, scalar2=1.0, op0=ALU.mult, op1=ALU.add), r=["lb"], w=["oml"])
          P.dve(lambda e: e.tensor_scalar(out=gc[:, 0:1], in0=gc[:, 0:1], scalar1=0.125, scalar2=None, op0=ALU.mult), r=["gc"], w=["gc"])
          P.dve(lambda e: e.tensor_scalar(out=gc[:, 2:3], in0=gc[:, 2:3], scalar1=0.125, scalar2=None, op0=ALU.mult), r=["gc"], w=["gc"])

          def rms_to_hb(src, srck, gtile, gk, dst, dstk):
              P.act(lambda e: e.activation(out=junk[:], in_=src, func=AF.Square, accum_out=ss[:, 0:1]), r=[srck], w=["junk", "ss"])
              P.act(lambda e: e.activation(out=ss[:, 1:2], in_=ss[:, 0:1], func=AF.Sqrt, bias=epsb[:, 0:1], scale=1.0 / 1024), r=["ss", "epsb"], w=["ss"])
              P.dve(lambda e: e.reciprocal(out=ss[:, 1:2], in_=ss[:, 1:2]), r=["ss"], w=["ss"])
              P.dve(lambda e: e.scalar_tensor_tensor(out=dst, in0=src, scalar=ss[:, 1:2], in1=gtile, op0=ALU.mult, op1=ALU.mult), r=[srck, "ss", gk], w=[dstk])

          def transpose8(srcb, srck, dst3, dstk):
              for kc in range(8):
                  P.pe(lambda e, kc=kc: e.transpose(ptr[:, kc * 128:(kc + 1) * 128], srcb[:, kc * 128:(kc + 1) * 128], idb[:]), r=[srck, "idb"], w=["ptr"])
              P.act(lambda e: e.copy(out=dst3, in_=ptr[:].rearrange("p (k t) -> p k t", k=8)), r=["ptr"], w=[dstk])

          def head_norm(psrc, psk, gcol, dst, dstk, tmpa, tmpb, n=512):
              P.act(lambda e: e.activation(out=tmpa, in_=psrc, func=AF.Square), r=[psk], w=["hn_a"])
              P.pe(lambda e: e.matmul(pw[:, 0:n], bd[:], tmpa, start=True, stop=True), r=["hn_a", "bd"], w=["pw"])
              P.act(lambda e: e.activation(out=tmpb, in_=pw[:, 0:n], func=AF.Sqrt, bias=epsb[:, 0:1], scale=1.0), r=["pw", "epsb"], w=["hn_b"])
              P.dve(lambda e: e.reciprocal(out=tmpb, in_=tmpb), r=["hn_b"], w=["hn_b"])
              P.dve(lambda e: e.scalar_tensor_tensor(out=dst, in0=psrc, scalar=gc[:, gcol:gcol + 1], in1=tmpb, op0=ALU.mult, op1=ALU.mult), r=[psk, "hn_b", "gc"], w=[dstk])

          hn_a = sb("hn_a", [128, 512], BF16); hn_b = sb("hn_b", [128, 512])
          chk(1)
          wkv = sb("wkv", [128, 8, 512], BF16)
          memT = sb("memT", [128, 8, 256], BF16)
          kmT = sb("kmT", [128, 2, 256], BF16)
          vm1 = sb("vm1", [128, 2, 4, 80], BF16)
          for kc in range(8):
              P.dma(lambda e, kc=kc: e.dma_start(out=wkv[:, kc, :], in_=w_kv[kc * 128:(kc + 1) * 128, :]), w=[f"wkv{kc}"], q="pool")
          for mt in range(2):
              P.dma(lambda e, mt=mt: e.dma_start(out=xin[mt][:], in_=mem[mt * 128:(mt + 1) * 128, :]), w=[f"xin{mt}"])
              rms_to_hb(xin[mt][:], f"xin{mt}", gmemb[:], "gmemb", hb[mt][:], f"hb{mt}")
              transpose8(hb[mt], f"hb{mt}", memT[:, :, mt * 128:(mt + 1) * 128], "memT")
          for hp in range(2):
              for kc in range(8):
                  P.pe(lambda e, kc=kc, hp=hp: e.matmul(pf[:, 0:256], wkv[:, kc, hp * 128:(hp + 1) * 128], memT[:, kc, :], start=(kc == 0), stop=(kc == 7)), r=[f"wkv{kc}", "memT"], w=["pf"])
              head_norm(pf[:, 0:256], "pf", 3, kmT[:, hp, :], "kmT", hn_a[:, 0:256], hn_b[:, 0:256], n=256)
          P.dve(lambda e: e.memset(vm1[:], 1.0), w=["vm1"])
          for mt in range(2):
              for kc in range(8):
                  P.pe(lambda e, kc=kc, mt=mt: e.matmul(pt[:, 0:256], memT[:, kc, mt * 128:(mt + 1) * 128], wkv[:, kc, 256:512], start=(kc == 0), stop=(kc == 7)), r=[f"wkv{kc}", "memT"], w=["pt"])
              P.act(lambda e, mt=mt: e.copy(out=vm1[:, mt, :, 0:64], in_=pt[:, 0:256].rearrange("p (h d) -> p h d", h=4)), r=["pt"], w=["vm1"])

          chk(2)
          def w32(name): return sb(name, [128, 512])
          def w16(name): return sb(name, [128, 512], BF16)
          sig = w32("sig"); fg = w32("fg"); lf = w32("lf"); bcum = w32("bcum"); qs = w32("qs"); kk = w32("kk")
          e1 = w32("e1"); e2 = w32("e2"); eb = w32("eb")
          qt = w16("qt"); kt = w16("kt"); qh = w16("qh")
          aall = sb("aall", [128, NG * 4 * 8])
          sm = sb("sm", [128, 5, 8])
          AT = sb("AT", [64, 512], BF16); ktok = sb("ktok", [64, 8, 128], BF16)
          vtok = sb("vtok", [64, 8, 512], BF16)
          oist = sb("oist", [64, 8, 128]); kvst = sb("kvst", [128, 8, 128])
          gst = sb("gst", [128, 512]); vst = sb("vst", [128, 4, 80], BF16); iwst = sb("iwst", [128, 128])
          qn = w16("qn"); mqT = sb("mqT", [128, 2, 512], BF16)
          iqst = sb("iqst", [64, 8, 512], BF16); ikst = sb("ikst", [64, 512], BF16)
          PT = sb("PT", [128, 2, 4, 512], BF16)
          rc = sb("rc", [128, 4]); omst = sb("omst", [128, 4, 64])
          P.dve(lambda e: e.memset(vst[:], 1.0), w=["vst"])
          P.dve(lambda e: e.memset(iwst[:], 0.0), w=["iwst"])

          def proj_f(col0, ncols, bank=pf, bk="pf"):
              for kc in range(8):
                  P.pe(lambda e, kc=kc: e.matmul(bank[0:ncols, :], wbf[:, kc, col0:col0 + ncols], hT[:, kc, :], start=(kc == 0), stop=(kc == 7)), r=[f"wbf{kc}", "hT"], w=[bk])

          def proj_t(t0, m, col0, ncols, out_ap, bk):
              for kc in range(8):
                  P.pe(lambda e, kc=kc: e.matmul(out_ap, hT[:, kc, t0:t0 + m], wbf[:, kc, col0:col0 + ncols], start=(kc == 0), stop=(kc == 7)), r=[f"wbf{kc}", "hT"], w=[bk])

          for g in range(NG):
              T0 = g * 512
              for blk in range(4):
                  xi = blk % 2
                  P.dma(lambda e, xi=xi, blk=blk, T0=T0: e.dma_start(out=xin[xi][:], in_=x[T0 + blk * 128:T0 + (blk + 1) * 128, :]), w=[f"xin{xi}"])
                  rms_to_hb(xin[xi][:], f"xin{xi}", gainb[:], "gainb", hb[xi][:], f"hb{xi}")
                  transpose8(hb[xi], f"hb{xi}", hT[:, :, blk * 128:(blk + 1) * 128], "hT")
              chk(3)
              for c in range(8):
                  proj_t(c * 64, 64, C_HI, 512, pt[0:64, :], "pt")
                  P.act(lambda e, c=c: e.copy(out=vtok[:, c, :], in_=pt[0:64, :]), r=["pt"], w=["vtok"])
              chk(4)
              import os
              for h in range(0 if os.environ.get('SKIP4') else 4):
                  proj_f(C_HF + h * 128, 128)
                  P.act(lambda e: e.activation(out=sig[:], in_=pf[:], func=AF.Sigmoid), r=["pf"], w=["sig"])
                  P.dve(lambda e, h=h: e.tensor_scalar(out=fg[:], in0=sig[:], scalar1=oml[:, h:h + 1], scalar2=lb[:, h:h + 1], op0=ALU.mult, op1=ALU.add), r=["sig", "oml", "lb"], w=["fg"])
                  P.act(lambda e: e.activation(out=lf[:], in_=fg[:], func=AF.Ln), r=["fg"], w=["lf"])
                  P.dve(lambda e: e.tensor_scalar(out=kk[:], in0=fg[:], scalar1=-1.0, scalar2=1.0, op0=ALU.mult, op1=ALU.add), r=["fg"], w=["kk"])
                  proj_f(C_HQ + h * 128, 128)
                  P.act(lambda e: e.activation(out=qs[:], in_=pf[:], func=AF.Silu), r=["pf"], w=["qs"])
                  P.dve(lambda e: e.tensor_tensor_scan(out=bcum[:], data0=rmask[:], data1=lf[:], initial=0.0, op0=ALU.mult, op1=ALU.add), r=["rmask", "lf"], w=["bcum"])
                  b3 = bcum[:].rearrange("p (c t) -> p c t", t=64)
                  P.dve(lambda e: e.tensor_scalar(out=sm[:, 0, :], in0=b3[:, :, 31], scalar1=-1.0, scalar2=None, op0=ALU.mult), r=["bcum"], w=["sm0"])
                  P.dve(lambda e: e.tensor_copy(out=sm[:, 1, :], in_=b3[:, :, 31]), r=["bcum"], w=["sm1"])
                  P.dve(lambda e: e.tensor_copy(out=sm[:, 2, :], in_=b3[:, :, 63]), r=["bcum"], w=["sm2"])
                  P.dve(lambda e: e.tensor_sub(out=sm[:, 3, :], in0=sm[:, 2, :], in1=sm[:, 1, :]), r=["sm1", "sm2"], w=["sm3"])
                  for c in range(8):
                      P.act(lambda e, c=c: e.activation(out=e1[:, c * 64:(c + 1) * 64], in_=bcum[:, c * 64:(c + 1) * 64], func=AF.Exp, bias=sm[:, 0, c:c + 1], scale=1.0), r=["bcum", "sm0"], w=["e1"])
                      P.act(lambda e, c=c: e.activation(out=e2[:, c * 64:(c + 1) * 64], in_=bcum[:, c * 64:(c + 1) * 64], func=AF.Exp, bias=sm[:, 1, c:c + 1], scale=-1.0), r=["bcum", "sm1"], w=["e2"])
                  P.act(lambda e: e.activation(out=eb[:], in_=bcum[:], func=AF.Exp), r=["bcum"], w=["eb"])
                  P.act(lambda e: e.activation(out=sm[:, 3, :], in_=sm[:, 3, :], func=AF.Exp), r=["sm3"], w=["sm3"])
                  P.act(lambda e: e.activation(out=sm[:, 4, :], in_=sm[:, 2, :], func=AF.Exp), r=["sm2"], w=["sm4"])
                  P.dve(lambda e: e.tensor_mul(out=qt[:], in0=qs[:], in1=e1[:]), r=["qs", "e1"], w=["qt"])
                  P.dve(lambda e: e.tensor_mul(out=kt[:], in0=kk[:], in1=e2[:]), r=["kk", "e2"], w=["kt"])
                  P.dve(lambda e: e.tensor_mul(out=qh[:], in0=qs[:], in1=eb[:]), r=["qs", "eb"], w=["qh"])
                  P.dma(lambda e, h=h, T0=T0: e.dma_start(out=qhat[h, :, T0:T0 + 512], in_=qh[:]), r=["qh"])
                  P.dve(lambda e, h=h, g=g: e.tensor_copy(out=aall[:, (g * 4 + h) * 8:(g * 4 + h + 1) * 8], in_=sm[:, 4, :]), r=["sm4"], w=["aall"])
                  for c in range(8):
                      P.pe(lambda e, c=c: e.matmul(pw[0:64, c * 64:(c + 1) * 64], kt[:, c * 64:(c + 1) * 64], qt[:, c * 64:(c + 1) * 64], start=True, stop=True), r=["kt", "qt"], w=["pw"])
                  P.dve(lambda e: e.tensor_tensor(out=AT[:], in0=pw[0:64, :], in1=cmask[:], op=ALU.mult), r=["pw", "cmask"], w=["AT"])
                  for c in range(8):
                      P.pe(lambda e, c=c: e.transpose(ptr[0:64, c * 128:(c + 1) * 128], kt[:, c * 64:(c + 1) * 64], idb[:]), r=["kt", "idb"], w=["ptr"])
                  P.act(lambda e: e.copy(out=ktok[:], in_=ptr[0:64, :].rearrange("p (c k) -> p c k", c=8)), r=["ptr"], w=["ktok"])
                  for c in range(8):
                      P.pe(lambda e, c=c, h=h: e.matmul(pbA[0:64, c * 128:(c + 1) * 128], AT[:, c * 64:(c + 1) * 64], vtok[:, c, h * 128:(h + 1) * 128], start=True, stop=True), r=["AT", "vtok"], w=["pbA"])
                  P.act(lambda e: e.copy(out=oist[:], in_=pbA[0:64, :].rearrange("p (c v) -> p c v", c=8)), r=["pbA"], w=["oist"])
                  P.dma(lambda e, h=h, g=g: e.dma_start(out=o_intra[g, h, :, :], in_=oist[:].rearrange("p c v -> p (c v)")), r=["oist"])
                  for c in range(8):
                      P.pe(lambda e, c=c, h=h: e.matmul(pbB[:, c * 128:(c + 1) * 128], ktok[:, c, :], vtok[:, c, h * 128:(h + 1) * 128], start=True, stop=True), r=["ktok", "vtok"], w=["pbB"])
                  P.dve(lambda e: e.tensor_tensor(out=kvst[:], in0=pbB[:].rearrange("p (c v) -> p c v", c=8), in1=sm[:, 3, :].unsqueeze(2).to_broadcast([128, 8, 128]), op=ALU.mult), r=["pbB", "sm3"], w=["kvst"])
                  P.dma(lambda e, h=h, g=g: e.dma_start(out=kv_out[g, h, :, :], in_=kvst[:].rearrange("p c v -> p (c v)")), r=["kvst"], q="pool")
              chk(5)
              for blk in range(4):
                  t0 = blk * 128
                  proj_t(t0, 128, C_HG, 512, pt[:, :], "pt")
                  P.act(lambda e: e.copy(out=gst[:], in_=pt[:]), r=["pt"], w=["gst"])
                  P.dma(lambda e, t0=t0, T0=T0: e.dma_start(out=g_out[T0 + t0:T0 + t0 + 128, :], in_=gst[:]), r=["gst"])
                  chk(5.1)
                  proj_t(t0, 128, C_SV, 256, pt[:, 0:256], "pt")
                  proj_t(t0, 128, C_IK, 72, pt[:, 256:328], "pt")
                  chk(5.2)
                  P.act(lambda e: e.copy(out=vst[:, :, 0:64], in_=pt[:, 0:256].rearrange("p (h d) -> p h d", h=4)), r=["pt"], w=["vst"])
                  P.act(lambda e: e.copy(out=iwst[:, 0:8], in_=pt[:, 320:328]), r=["pt"], w=["iwst"])
                  chk(5.3)
                  P.dma(lambda e, t0=t0, T0=T0: e.dma_start(out=V[T0 + t0:T0 + t0 + 128, :], in_=vst[:].rearrange("p h e -> p (h e)")), r=["vst"])
                  P.dma(lambda e, t0=t0, T0=T0: e.dma_start(out=iw_out[T0 + t0:T0 + t0 + 128, :], in_=iwst[:]), r=["iwst"])
              chk(6)
              for pair in range(2):
                  proj_f(C_SQ + pair * 128, 128)
                  head_norm(pf[:], "pf", 0, qn[:], "qn", hn_a[:], hn_b[:])
                  P.dma(lambda e, pair=pair, T0=T0: e.dma_start(out=QT[pair, :, T0:T0 + 512], in_=qn[:]), r=["qn"])
                  proj_f(C_SK + pair * 128, 128)
                  head_norm(pf[:], "pf", 1, qn[:], "qn", hn_a[:], hn_b[:])
                  P.dma(lambda e, pair=pair, T0=T0: e.dma_start(out=KT[pair, :, T0:T0 + 512], in_=qn[:]), r=["qn"])
                  proj_f(C_MQ + pair * 128, 128)
                  head_norm(pf[:], "pf", 2, mqT[:, pair, :], "mqT", hn_a[:], hn_b[:])
              for ih in range(8):
                  proj_f(C_IQ + ih * 64, 64)
                  P.act(lambda e, ih=ih: e.copy(out=iqst[:, ih, :], in_=pf[0:64, :]), r=["pf"], w=["iqst"])
              P.dma(lambda e, T0=T0: e.dma_start(out=iqT[:, :, T0:T0 + 512], in_=iqst[:]), r=["iqst"])
              proj_f(C_IK, 64)
              P.act(lambda e: e.copy(out=ikst[:], in_=pf[0:64, :]), r=["pf"], w=["ikst"])
              P.dma(lambda e, T0=T0: e.dma_start(out=ikT[:, T0:T0 + 512], in_=ikst[:]), r=["ikst"])
              chk(7)
              for mt in range(2):
                  for h in range(4):
                      po = (h % 2) * 64
                      P.pe(lambda e, mt=mt, h=h, po=po: e.matmul(pw[:], kmT[po:po + 64, h // 2, mt * 128:(mt + 1) * 128], mqT[po:po + 64, h // 2, :], start=True, stop=True), r=["kmT", "mqT"], w=["pw"])
                      P.act(lambda e, mt=mt, h=h: e.activation(out=PT[:, mt, h, :], in_=pw[:], func=AF.Exp), r=["pw"], w=["PT"])
              for blk in range(4):
                  t0 = blk * 128
                  for h in range(4):
                      for mt in range(2):
                          P.pe(lambda e, mt=mt, h=h, t0=t0: e.matmul(pt[:, h * 128:h * 128 + 65], PT[:, mt, h, t0:t0 + 128], vm1[:, mt, h, 0:65], start=(mt == 0), stop=(mt == 1)), r=["PT", "vm1"], w=["pt"])
                  pv = pt[:].rearrange("p (h e) -> p h e", e=128)
                  P.dve(lambda e, pv=pv: e.reciprocal(out=rc[:], in_=pv[:, :, 64]), r=["pt"], w=["rc"])
                  P.dve(lambda e, pv=pv: e.tensor_tensor(out=omst[:], in0=pv[:, :, 0:64], in1=rc[:].unsqueeze(2).to_broadcast([128, 4, 64]), op=ALU.mult), r=["pt", "rc"], w=["omst"])
                  P.dma(lambda e, t0=t0, T0=T0: e.dma_start(out=omem[T0 + t0:T0 + t0 + 128, :], in_=omst[:].rearrange("p h d -> p (h d)")), r=["omst"])
      except _Stop:
        pass
      P.dma(lambda e: e.dma_start(out=a_out[:, :], in_=aall[:]), r=["aall"])
      P.emit()
    return nc


def consts_A():
    ident = np.eye(128, dtype=np.float32)
    bd = np.zeros((128, 128), np.float32); bd[:64, :64] = 1.0 / 64; bd[64:, 64:] = 1.0 / 64
    cm = np.triu(np.ones((64, 64), np.float32))
    cmask = np.tile(cm, (1, 8))
    rmask = np.ones((128, 512), np.float32); rmask[:, ::64] = 0.0
    return dict(ident=ident, bd=bd, cmask=cmask, rmask=rmask)

NSLOT = 16
NITER = 20
RANGE = 256.0


def build_B(NS=NSLOT):
    nc = bass.Bass("TRN2", target_bir_lowering=False)
    def din(name, shape, dt=F32): return nc.dram_tensor(name, shape, dt, kind="ExternalInput").ap()
    def dout(name, shape, dt=F32): return nc.dram_tensor(name, shape, dt, kind="ExternalOutput").ap()
    QT = din("QT", [128, 2, 2048], BF16)
    iqs = din("iqs", [16, 64, 1024], BF16)
    iwp = din("iwp", [128, 128])
    KTg = din("KTg", [128, 2, 16384], BF16)
    Vg = din("Vg", [128, 128, 320], BF16)
    ikT = din("ikT", [64, 16384], BF16)
    CB = din("CB", [128, 1024])
    ident = din("ident", [128, 128])
    a_s = din("a_s", [128, 256]); kv_s = din("kv_s", [128, 64, 256])
    S_out = dout("S_out", [128, 64, 256])
    o_sa = dout("o_sa", [2048, 256])
    P = Prog(nc)
    with contextlib.ExitStack() as st:
        def sb(name, shape, dt=F32): return st.enter_context(nc.sbuf_tensor(name, shape, dt))
        def ps(name, shape, dt=F32): return st.enter_context(nc.psum_tensor(name, shape, dt))
        score = sb("score", [128, 16384])
        junk = sb("junk", [128, 4096], BF16)
        ikt = sb("ikt", [64, 16384], BF16)
        qt = sb("qt", [128, 2, 2048], BF16)
        iq = sb("iq", [64, 1024], BF16)
        iwt = sb("iwt", [128, 128]); sgn = sb("sgn", [128, 128]); absw = sb("absw", [128, 128])
        cb = sb("cb", [128, 1024]); idb = sb("idb", [128, 128], BF16)
        D = sb("D", [128, 8, 128], BF16)
        T = [sb(f"T{h}", [128, 512], BF16) for h in range(8)]
        ktcs = [sb(f"ktc{i}", [128, 2, 2048], BF16) for i in range(2)]; vcs = [sb(f"vc{i}", [128, 16, 320], BF16) for i in range(2)]
        mbcs = [sb(f"mbc{i}", [128, 512], BF16) for i in range(2)]; pTs = [sb(f"pT{i}", [128, 512], BF16) for i in range(2)]
        sm = sb("sm", [128, 16]); cnt4 = sb("cnt4", [128, 4])
        ost = sb("ost", [128, 4, 64]); rc = sb("rc", [128, 4])
        asb = sb("asb", [128, 256])
        zl = sb("zl", [128, 128], BF16); zr = sb("zr", [128, 512], BF16)
        P.dve(lambda e: e.memset(zl[:], 0.0), w=["zl"])
        P.dve(lambda e: e.memset(zr[:], 0.0), w=["zr"])
        pS = [ps(f"pS{i}", [128, 512]) for i in range(3)]
        pSc = ps("pSc", [128, 512])
        pL = [ps(f"pL{i}", [128, 512]) for i in range(2)]
        pO = ps("pO", [128, 512])
        P.dma(lambda e: e.dma_start(out=ikt[:], in_=ikT[:, :]), w=["ikt"])
        P.dma(lambda e: e.dma_start(out=qt[:], in_=QT[:, :, :]), w=["qt"])
        P.dma(lambda e: e.dma_start(out=iwt[:], in_=iwp[:, :]), w=["iwt"])
        P.dma(lambda e: e.dma_start(out=cb[:], in_=CB[:, :]), w=["cb"])
        P.dma(lambda e: e.dma_start(out=idb[:], in_=ident[:, :]), w=["idb"], q="pool")
        P.dma(lambda e: e.dma_start(out=asb[:], in_=a_s[:, :]), w=["asb"])
        for half in range(2):
            kin = score[:, 0:8192].rearrange("p (v c) -> p v c", c=256)
            kout = score[:, 8192:16384].rearrange("p (v c) -> p v c", c=256)
            P.dma(lambda e, half=half, kin=kin: e.dma_start(out=kin, in_=kv_s[:, half * 32:(half + 1) * 32, :]), w=["score"])
            for v in range(32):
                P.dve(lambda e, v=v: e.tensor_tensor_scan(out=score[:, 8192 + v * 256:8192 + (v + 1) * 256], data0=asb[:], data1=score[:, v * 256:(v + 1) * 256], initial=0.0, op0=ALU.mult, op1=ALU.add), r=["asb", "score"], w=["score"])
            P.dma(lambda e, half=half, kout=kout: e.dma_start(out=S_out[:, half * 32:(half + 1) * 32, :], in_=kout), r=["score"])
        P.act(lambda e: e.activation(out=sgn[:], in_=iwt[:], func=AF.Sign), r=["iwt"], w=["sgn"])
        P.dve(lambda e: e.tensor_mul(out=absw[:], in0=iwt[:], in1=sgn[:]), r=["iwt", "sgn"], w=["absw"])
        for i in range(NS):
            nk = 8 * (i + 1); nch = 2 * (i + 1); n = nk * 128
            q0 = i * 128
            P.dma(lambda e, i=i: e.dma_start(out=iq[:], in_=iqs[i, :, :]), w=["iq"])
            for h in range(8):
                P.dve(lambda e, h=h, i=i: e.tensor_scalar(out=D[:, h, :], in0=idb[:], scalar1=sgn[:, i * 8 + h:i * 8 + h + 1], scalar2=None, op0=ALU.mult), r=["idb", "sgn"], w=["D"])
            for kc in range(nch):
                for h in range(8):
                    b = (kc * 8 + h) % 3
                    P.pe(lambda e, h=h, kc=kc, b=b: e.matmul(pS[b][:], iq[:, h * 128:(h + 1) * 128], ikt[:, kc * 512:(kc + 1) * 512], start=True, stop=True), r=["iq", "ikt"], w=[f"pS{b}"])
                    col = i * 8 + h
                    if h % 2 == 0:
                        P.act(lambda e, h=h, b=b, col=col: e.activation(out=T[h][:], in_=pS[b][:], func=AF.Relu, scale=absw[:, col:col + 1]), r=[f"pS{b}", "absw"], w=[f"T{h}"])
                    else:
                        P.dve(lambda e, h=h, b=b, col=col: e.tensor_scalar(out=T[h][:], in0=pS[b][:], scalar1=absw[:, col:col + 1], scalar2=0.0, op0=ALU.mult, op1=ALU.max), r=[f"pS{b}", "absw"], w=[f"T{h}"])
                    P.pe(lambda e, h=h: e.matmul(pSc[:], D[:, h, :], T[h][:], start=(h == 0), stop=(h == 7)), r=["D", f"T{h}"], w=["pSc"])
                if kc >= nch - 2:
                    j = kc - (nch - 2)
                    P.dve(lambda e, kc=kc, j=j: e.tensor_tensor(out=score[:, kc * 512:(kc + 1) * 512], in0=pSc[:], in1=cb[:, j * 512:(j + 1) * 512], op=ALU.add), r=["pSc", "cb"], w=["score"])
                else:
                    P.dve(lambda e, kc=kc: e.tensor_copy(out=score[:, kc * 512:(kc + 1) * 512], in_=pSc[:]), r=["pSc"], w=["score"])
            P.dve(lambda e, n=n: e.reduce_max(out=sm[:, 1:2], in_=score[:, 0:n], axis=AX.X), r=["score"], w=["sm"])
            P.dve(lambda e: e.tensor_scalar(out=sm[:, 0:1], in0=sm[:, 1:2], scalar1=-RANGE, scalar2=None, op0=ALU.add), r=["sm"], w=["sm"])
            npc = (n + 4095) // 4096
            for it in range(NITER):
                wdt = RANGE / (2.0 ** (it + 1))
                P.dve(lambda e, wdt=wdt: e.tensor_scalar(out=sm[:, 2:3], in0=sm[:, 0:1], scalar1=wdt, scalar2=None, op0=ALU.add), r=["sm"], w=["sm"])
                P.dve(lambda e: e.memset(cnt4[:], 0.0), w=["cnt4"])
                for pc in range(npc):
                    c0 = pc * 4096; c1 = min(n, c0 + 4096)
                    P.dve(lambda e, c0=c0, c1=c1, pc=pc: e.tensor_scalar(out=junk[:, 0:c1 - c0], in0=score[:, c0:c1], scalar1=sm[:, 2:3], scalar2=0.0, op0=ALU.is_gt, op1=ALU.add, accum_out=cnt4[:, pc:pc + 1]), r=["score", "sm"], w=["junk", "cnt4"])
                P.dve(lambda e, npc=npc: e.reduce_sum(out=sm[:, 3:4], in_=cnt4[:, 0:npc], axis=AX.X), r=["cnt4"], w=["sm"])
                P.dve(lambda e, wdt=wdt: e.tensor_scalar(out=sm[:, 4:5], in0=sm[:, 3:4], scalar1=255.5, scalar2=wdt, op0=ALU.is_gt, op1=ALU.mult), r=["sm"], w=["sm"])
                P.dve(lambda e: e.tensor_add(out=sm[:, 0:1], in0=sm[:, 0:1], in1=sm[:, 4:5]), r=["sm"], w=["sm"])
            P.pe(lambda e: e.matmul(pO[:], zl[:], zr[:], start=True, stop=False), r=["zl", "zr"], w=["pO"])
            for kc in range(nch):
                wi = (kc // 4) % 2
                ktc = ktcs[wi]; vc = vcs[wi]; mbc = mbcs[kc % 2]
                if kc % 4 == 0:
                    P.dma(lambda e, kc=kc, ktc=ktc: e.dma_start(out=ktc[:], in_=KTg[:, :, kc * 512:kc * 512 + 2048]), w=[f"ktc{wi}"])
                    P.dma(lambda e, kc=kc, vc=vc: e.dma_start(out=vc[:], in_=Vg[:, kc * 4:kc * 4 + 16, :]), w=[f"vc{wi}"], q="pool")
                P.dve(lambda e, kc=kc, mbc=mbc: e.tensor_scalar(out=mbc[:], in0=score[:, kc * 512:(kc + 1) * 512], scalar1=sm[:, 0:1], scalar2=-30000.0, op0=ALU.is_le, op1=ALU.mult), r=["score", "sm"], w=[f"mbc{kc % 2}"])
                for t in range(4):
                    kt = kc * 4 + t
                    lt = (kc % 4) * 4 + t
                    b = kt % 2
                    pT = pTs[b]
                    for h in range(4):
                        po = (h % 2) * 64
                        P.pe(lambda e, h=h, po=po, lt=lt, b=b, q0=q0, ktc=ktc: e.matmul(pL[b][:, h * 128:(h + 1) * 128], ktc[po:po + 64, h // 2, lt * 128:(lt + 1) * 128], qt[po:po + 64, h // 2, q0:q0 + 128], start=True, stop=False), r=[f"ktc{wi}", "qt"], w=[f"pL{b}"])
                        P.pe(lambda e, h=h, t=t, b=b, mbc=mbc: e.matmul(pL[b][:, h * 128:(h + 1) * 128], mbc[:, t * 128:(t + 1) * 128], idb[:], start=False, stop=True), r=[f"mbc{kc % 2}", "idb"], w=[f"pL{b}"])
                    P.act(lambda e, b=b, pT=pT: e.activation(out=pT[:], in_=pL[b][:], func=AF.Exp), r=[f"pL{b}"], w=[f"pT{b}"])
                    for h in range(4):
                        P.pe(lambda e, h=h, lt=lt, kt=kt, nk=nk, pT=pT, vc=vc: e.matmul(pO[:, h * 128:h * 128 + 65], pT[:, h * 128:(h + 1) * 128], vc[:, lt, h * 80:h * 80 + 65], start=False, stop=(kt == nk - 1)), r=[f"pT{b}", f"vc{wi}"], w=["pO"])
            pv = pO[:].rearrange("p (h e) -> p h e", e=128)
            P.dve(lambda e, pv=pv: e.reciprocal(out=rc[:], in_=pv[:, :, 64]), r=["pO"], w=["rc"])
            P.dve(lambda e, pv=pv: e.tensor_tensor(out=ost[:], in0=pv[:, :, 0:64], in1=rc[:].unsqueeze(2).to_broadcast([128, 4, 64]), op=ALU.mult), r=["pO", "rc"], w=["ost"])
            P.dma(lambda e, q0=q0: e.dma_start(out=o_sa[q0:q0 + 128, :], in_=ost[:].rearrange("p h d -> p (h d)")), r=["ost"])
        P.emit()
    return nc


EPS = 1e-6
NT = 2048
NG = 4


def build_C():
    nc = bass.Bass("TRN2", target_bir_lowering=False)
    def din(name, shape, dt=F32): return nc.dram_tensor(name, shape, dt, kind="ExternalInput").ap()
    def dout(name, shape, dt=F32): return nc.dram_tensor(name, shape, dt, kind="ExternalOutput").ap()
    x = din("x", [NT, 1024]); o_intra = din("o_intra", [NG, 4, 64, 1024]); qhat = din("qhat", [4, 128, NT], BF16)
    Sp = din("Sp", [NG, 128, 4096]); g_in = din("g_in", [NT, 512]); omem = din("omem", [NT, 256]); o_sa = din("o_sa", [NT, 256])
    w_out = din("w_out", [128, 8192]); w_fi = din("w_fi", [44, 128, 1024]); w_fo = din("w_fo", [128, 22 * 1024])
    gffn = din("gffn", [128, 1024]); ghg = din("ghg", [64, 512]); ident = din("ident", [128, 128])
    x_out = dout("x_out", [NT, 1024])
    P = Prog(nc)
    with contextlib.ExitStack() as st:
        def sb(name, shape, dt=F32): return st.enter_context(nc.sbuf_tensor(name, shape, dt))
        def ps(name, shape, dt=F32): return st.enter_context(nc.psum_tensor(name, shape, dt))
        wout = sb("wout", [128, 8, 1024], BF16); wfo = sb("wfo", [128, 22, 1024], BF16)
        wt = [sb(f"wt{i}", [128, 8, 128], BF16) for i in range(4)]
        gf = sb("gf", [128, 1024]); gh = sb("gh", [64, 4, 128]); idb = sb("idb", [128, 128], BF16)
        uT = sb("uT", [128, 22, 512], BF16)
        xg = sb("xg", [128, 4, 1024])
        h2T = sb("h2T", [128, 8, 512], BF16); mixT = sb("mixT", [128, 8, 512], BF16)
        qh = sb("qh", [128, 4, 512], BF16); spg = sb("spg", [128, 4, 8, 128], BF16)
        oig = sb("oig", [64, 4, 8, 128])
        o32 = sb("o32", [64, 4, 128]); gch = sb("gch", [64, 512]); mixb = sb("mixb", [64, 1024], BF16)
        junk = sb("junk", [128, 1024], BF16); epsb = sb("epsb", [128, 1]); P.dve(lambda e: e.memset(epsb[:], EPS), w=["epsb"]); ss = sb("ss", [128, 8])
        hb = sb("hb", [128, 1024], BF16); sa = sb("sa", [128, 512])
        ptr = ps("ptr", [128, 1024], BF16)
        pf = ps("pf", [128, 512]); pw = ps("pw", [128, 512]); pt = ps("pt", [128, 512]); po = ps("po", [128, 512])
        for q4 in range(4):
            P.dma(lambda e, q4=q4: e.dma_start(out=wout[:, q4 * 2:(q4 + 1) * 2, :], in_=w_out[:, q4 * 2048:(q4 + 1) * 2048]), w=[f"wout{q4}"], q="pool")
        for q11 in range(11):
            P.dma(lambda e, q11=q11: e.dma_start(out=wfo[:, q11 * 2:(q11 + 1) * 2, :], in_=w_fo[:, q11 * 2048:(q11 + 1) * 2048]), w=[f"wfo{q11}"], q="pool")
        P.dma(lambda e: e.dma_start(out=gf[:], in_=gffn[:, :]), w=["gf"])
        P.dma(lambda e: e.dma_start(out=gh[:], in_=ghg[:, :]), w=["gh"])
        P.dma(lambda e: e.dma_start(out=idb[:], in_=ident[:, :]), w=["idb"], q="pool")
        wti = 0
        for g in range(NG):
            T0 = g * 512
            for h in range(4):
                P.dma(lambda e, h=h, T0=T0: e.dma_start(out=qh[:, h, :], in_=qhat[h, :, T0:T0 + 512]), w=["qh"])
            P.dma(lambda e, g=g: e.dma_start(out=spg[:], in_=Sp[g, :, :]), w=["spg"], q="pool")
            P.dma(lambda e, g=g: e.dma_start(out=oig[:], in_=o_intra[g, :, :, :].rearrange("h p f -> p h f")), w=["oig"])
            for c in range(8):
                r0 = T0 + c * 64
                for h in range(4):
                    P.pe(lambda e, h=h, c=c: e.matmul(pw[0:64, h * 128:(h + 1) * 128], qh[:, h, c * 64:(c + 1) * 64], spg[:, h, c, :], start=True, stop=True), r=["qh", "spg"], w=["pw"])
                P.dve(lambda e, c=c: e.tensor_tensor(out=o32[:], in0=pw[0:64, :].rearrange("p (h v) -> p h v", h=4), in1=oig[:, :, c, :], op=ALU.add), r=["pw", "oig"], w=["o32"])
                for h in range(4):
                    P.act(lambda e, h=h: e.activation(out=junk[0:64, 0:128], in_=o32[:, h, :], func=AF.Square, accum_out=ss[0:64, h:h + 1]), r=["o32"], w=["junk", "ss"])
                P.act(lambda e: e.activation(out=ss[0:64, 4:8], in_=ss[0:64, 0:4], func=AF.Sqrt, bias=epsb[0:64, 0:1], scale=1.0 / 128), r=["ss", "epsb"], w=["ss"])
                P.dve(lambda e: e.reciprocal(out=ss[0:64, 4:8], in_=ss[0:64, 4:8]), r=["ss"], w=["ss"])
                P.dma(lambda e, r0=r0: e.dma_start(out=gch[:], in_=g_in[r0:r0 + 64, :]), w=["gch"])
                P.act(lambda e: e.activation(out=gch[:], in_=gch[:], func=AF.Silu), r=["gch"], w=["gch"])
                P.dve(lambda e: e.tensor_tensor(out=o32[:], in0=o32[:], in1=ss[0:64, 4:8].unsqueeze(2).to_broadcast([64, 4, 128]), op=ALU.mult), r=["o32", "ss"], w=["o32"])
                P.dve(lambda e: e.tensor_tensor(out=o32[:], in0=o32[:], in1=gh[:], op=ALU.mult), r=["o32", "gh"], w=["o32"])
                P.dma(lambda e, r0=r0: e.dma_start(out=mixb[:, 512:768], in_=o_sa[r0:r0 + 64, :]), w=["mixb"], q="pool")
                P.dma(lambda e, r0=r0: e.dma_start(out=mixb[:, 768:1024], in_=omem[r0:r0 + 64, :]), w=["mixb"], q="pool")
                P.dve(lambda e: e.tensor_tensor(out=mixb[:, 0:512], in0=o32[:].rearrange("p h v -> p (h v)"), in1=gch[:], op=ALU.mult), r=["o32", "gch"], w=["mixb"])
                for kc in range(8):
                    P.pe(lambda e, kc=kc: e.transpose(ptr[:, kc * 64:(kc + 1) * 64], mixb[:, kc * 128:(kc + 1) * 128], idb[0:64, 0:64]), r=["mixb", "idb"], w=["ptr"])
                P.act(lambda e, c=c: e.copy(out=mixT[:, :, c * 64:(c + 1) * 64], in_=ptr[:, 0:512].rearrange("p (k t) -> p k t", k=8)), r=["ptr"], w=["mixT"])
            for blk in range(4):
                r0 = T0 + blk * 128
                P.dma(lambda e, r0=r0, blk=blk: e.dma_start(out=xg[:, blk, :], in_=x[r0:r0 + 128, :]), w=[f"xg{blk}"])
                for half in range(2):
                    for kc in range(8):
                        P.pe(lambda e, kc=kc, half=half, blk=blk: e.matmul(pt[:], mixT[:, kc, blk * 128:(blk + 1) * 128], wout[:, kc, half * 512:(half + 1) * 512], start=(kc == 0), stop=(kc == 7)), r=["mixT", f"wout{kc // 2}"], w=["pt"])
                    P.dve(lambda e, half=half, blk=blk: e.tensor_tensor(out=xg[:, blk, half * 512:(half + 1) * 512], in0=xg[:, blk, half * 512:(half + 1) * 512], in1=pt[:], op=ALU.add), r=["pt", f"xg{blk}"], w=[f"xg{blk}"])
                P.act(lambda e, blk=blk: e.activation(out=junk[:], in_=xg[:, blk, :], func=AF.Square, accum_out=ss[:, 0:1]), r=[f"xg{blk}"], w=["junk", "ss"])
                P.act(lambda e: e.activation(out=ss[:, 1:2], in_=ss[:, 0:1], func=AF.Sqrt, bias=epsb[:, 0:1], scale=1.0 / 1024), r=["ss", "epsb"], w=["ss"])
                P.dve(lambda e: e.reciprocal(out=ss[:, 1:2], in_=ss[:, 1:2]), r=["ss"], w=["ss"])
                P.dve(lambda e, blk=blk: e.scalar_tensor_tensor(out=hb[:], in0=xg[:, blk, :], scalar=ss[:, 1:2], in1=gf[:], op0=ALU.mult, op1=ALU.mult), r=[f"xg{blk}", "ss", "gf"], w=["hb"])
                for kc in range(8):
                    P.pe(lambda e, kc=kc: e.transpose(ptr[:, kc * 128:(kc + 1) * 128], hb[:, kc * 128:(kc + 1) * 128], idb[:]), r=["hb", "idb"], w=["ptr"])
                P.act(lambda e, blk=blk: e.copy(out=h2T[:, :, blk * 128:(blk + 1) * 128], in_=ptr[:].rearrange("p (k t) -> p k t", k=8)), r=["ptr"], w=["h2T"])
            for j in range(22):
                ia = wti % 4; ib = (wti + 1) % 4; wti += 2
                P.dma(lambda e, j=j, ia=ia: e.dma_start(out=wt[ia][:], in_=w_fi[j, :, :]), w=[f"wt{ia}"], q="pool")
                P.dma(lambda e, j=j, ib=ib: e.dma_start(out=wt[ib][:], in_=w_fi[22 + j, :, :]), w=[f"wt{ib}"], q="pool")
                for kc in range(8):
                    P.pe(lambda e, kc=kc, ia=ia: e.matmul(pf[:], wt[ia][:, kc, :], h2T[:, kc, :], start=(kc == 0), stop=(kc == 7)), r=[f"wt{ia}", "h2T"], w=["pf"])
                for kc in range(8):
                    P.pe(lambda e, kc=kc, ib=ib: e.matmul(po[:], wt[ib][:, kc, :], h2T[:, kc, :], start=(kc == 0), stop=(kc == 7)), r=[f"wt{ib}", "h2T"], w=["po"])
                P.act(lambda e: e.activation(out=sa[:], in_=pf[:], func=AF.Silu), r=["pf"], w=["sa"])
                P.dve(lambda e, j=j: e.tensor_tensor(out=uT[:, j, :], in0=sa[:], in1=po[:], op=ALU.mult), r=["sa", "po"], w=["uT"])
            for blk in range(4):
                r0 = T0 + blk * 128
                for half in range(2):
                    for j in range(22):
                        P.pe(lambda e, j=j, half=half, blk=blk: e.matmul(pt[:], uT[:, j, blk * 128:(blk + 1) * 128], wfo[:, j, half * 512:(half + 1) * 512], start=(j == 0), stop=(j == 21)), r=["uT", f"wfo{j // 2}"], w=["pt"])
                    P.dve(lambda e, half=half, blk=blk: e.tensor_tensor(out=xg[:, blk, half * 512:(half + 1) * 512], in0=xg[:, blk, half * 512:(half + 1) * 512], in1=pt[:], op=ALU.add), r=["pt", f"xg{blk}"], w=[f"xg{blk}"])
                P.dma(lambda e, r0=r0, blk=blk: e.dma_start(out=x_out[r0:r0 + 128, :], in_=xg[:, blk, :]), r=[f"xg{blk}"])
        P.emit()
    return nc


CORES = list(range(8))


def _shard_tok(a, c):
    return np.ascontiguousarray(a.reshape(16, 8, 128, -1)[:, c].reshape(2048, -1))


def _c(a):
    return np.ascontiguousarray(a)


def make_CBs():
    CBs = []
    for c in range(8):
        cb = np.zeros((128, 8, 128), np.float32)
        cb[:, c + 1:, :] = -1e30
        tri = np.where(np.arange(128)[None, :] <= np.arange(128)[:, None], 0.0, -1e30).astype(np.float32)
        cb[:, c, :] = tri
        CBs.append(_c(cb.reshape(128, 1024)))
    return CBs


def make_A_inputs(inp, xs, layer, cA):
    base = dict(cA)
    base["w_in"] = _c(inp["w_in"][layer]); base["gmix"] = _c(np.broadcast_to(inp["norm_mix"][layer], (128, 1024)))
    base["mem"] = _c(inp["mem"][0]); base["gmem"] = _c(np.broadcast_to(inp["norm_mem"][layer], (128, 1024)))
    base["w_kv"] = _c(inp["w_mem_kv"][layer])
    base["lbl"] = _c(inp["lb_logits"].reshape(2, 4, 128).transpose(2, 0, 1).reshape(128, 8))
    base["lsel"] = np.full((128, 1), float(layer), np.float32)
    base["gcols"] = _c(np.stack([np.tile(inp["sa_q_gain"][layer], 2), np.tile(inp["sa_k_gain"][layer], 2),
                                 np.tile(inp["mem_q_gain"][layer], 2), np.tile(inp["mem_k_gain"][layer], 2)], axis=1))
    return [dict(base, x=xs[c]) for c in CORES]


def make_B_inputs(ra, CBs, ident):
    KTg = np.stack([np.asarray(ra[c]["KT"]).reshape(2, 128, 16, 128) for c in CORES], axis=3).reshape(2, 128, 16384)
    KTg = _c(KTg.transpose(1, 0, 2))
    Vg = np.stack([np.asarray(ra[c]["V"]).reshape(16, 128, 320) for c in CORES], axis=1).reshape(128, 128, 320)
    Vg = _c(Vg.transpose(1, 0, 2))
    ikg = _c(np.stack([np.asarray(ra[c]["ikT"]).reshape(64, 16, 128) for c in CORES], axis=2).reshape(64, 16384))
    a_glob = np.stack([np.asarray(ra[c]["a_out"]).reshape(128, 4, 4, 4, 2).transpose(0, 2, 1, 3, 4).reshape(128, 4, 16, 2) for c in CORES], axis=3)
    a_glob = a_glob.reshape(128, 4, 256).transpose(1, 0, 2)
    kv_glob = np.stack([np.asarray(ra[c]["kv_out"]).reshape(4, 4, 128, 4, 2, 128).transpose(1, 2, 0, 3, 4, 5).reshape(4, 128, 16, 2, 128) for c in CORES], axis=3)
    kv_glob = kv_glob.reshape(4, 128, 256, 128)
    maps = []
    for c in CORES:
        h, half = c // 2, c % 2
        d = dict(ident=ident, KTg=KTg, Vg=Vg, ikT=ikg, CB=CBs[c])
        d["QT"] = _c(np.asarray(ra[c]["QT"]).transpose(1, 0, 2))
        d["iqs"] = _c(np.asarray(ra[c]["iqT"]).reshape(64, 8, 16, 128).transpose(2, 0, 1, 3).reshape(16, 64, 1024))
        d["iwp"] = _c(np.asarray(ra[c]["iw_out"])[:, :8].reshape(16, 128, 8).transpose(1, 0, 2).reshape(128, 128))
        d["a_s"] = _c(a_glob[h])
        d["kv_s"] = _c(kv_glob[h][:, :, half * 64:(half + 1) * 64].transpose(0, 2, 1))
        maps.append(d)
    return maps


def make_C_inputs(inp, layer, xs, ra, rb, ident):
    Sg = np.stack([np.concatenate([np.asarray(rb[2 * h]["S_out"]), np.asarray(rb[2 * h + 1]["S_out"])], axis=1) for h in range(4)], axis=0)
    Sprev = np.concatenate([np.zeros_like(Sg[..., :1]), Sg[..., :-1]], axis=-1)
    Sr = Sprev.reshape(4, 128, 128, 16, 8, 2)
    wo = _c(inp["w_out"][layer].reshape(8, 128, 1024).transpose(1, 0, 2).reshape(128, 8192))
    wfi = _c(inp["w_ffn_in"][layer].reshape(8, 128, 44, 128).transpose(2, 1, 0, 3).reshape(44, 128, 1024))
    wfo = _c(inp["w_ffn_out"][layer].reshape(22, 128, 1024).transpose(1, 0, 2).reshape(128, 22 * 1024))
    gffn = _c(np.broadcast_to(inp["norm_ffn"][layer], (128, 1024)))
    ghg = _c(np.broadcast_to(np.tile(inp["hg_out_gain"][layer], 4), (64, 512)))
    maps = []
    for c in CORES:
        sp = Sr[:, :, :, :, c, :].reshape(4, 128, 128, 4, 4, 2).transpose(3, 1, 0, 4, 5, 2).reshape(4, 128, 4096)
        d = dict(x=xs[c], o_intra=np.asarray(ra[c]["o_intra"]), qhat=np.asarray(ra[c]["qhat"]), Sp=_c(sp),
                 g_in=np.asarray(ra[c]["g_out"]), omem=np.asarray(ra[c]["omem"]), o_sa=np.asarray(rb[c]["o_sa"]),
                 w_out=wo, w_fi=wfi, w_fo=wfo, gffn=gffn, ghg=ghg, ident=ident)
        maps.append(d)
    return maps


def kernel(**inp):
    inp = {k: np.asarray(v) for k, v in inp.items()}
    x = inp["x"][0]
    xs = [_shard_tok(x, c) for c in CORES]
    cA = consts_A()
    ident = cA["ident"]
    ncA = build_A(); ncB = build_B(); ncC = build_C()
    CBs = make_CBs()
    for layer in range(2):
        ra = run_bass_kernel_spmd(ncA, make_A_inputs(inp, xs, layer, cA), core_ids=CORES).results
        rb = run_bass_kernel_spmd(ncB, make_B_inputs(ra, CBs, ident), core_ids=CORES).results
        rc_ = run_bass_kernel_spmd(ncC, make_C_inputs(inp, layer, xs, ra, rb, ident), core_ids=CORES).results
        xs = [np.asarray(rc_[c]["x_out"]) for c in CORES]
    out = np.stack([xs[c].reshape(16, 128, 1024) for c in CORES], axis=1).reshape(1, 16384, 1024)
    return out.astype(np.float32)
```
